# Optimizing a Trainium2 kernel written in Bass

```python
import math
import jax, jax.numpy as jnp
from jax import lax
import numpy as np

D_MODEL = 1024
BATCH = 8
SEQ = 4096
DEPTH = 4

PLE_DIM = 256
HEAD_DIM = 64
N_GROUPS = 4
GROUP_WIDTH = D_MODEL // N_GROUPS
N_HEADS_G = GROUP_WIDTH // HEAD_DIM
MOBA_BLOCK = 256
MOBA_TOPK = 3
MOBA_QCHUNK = 64
ATTN_QBLOCK = 128
N_REL_BUCKETS = 32
REL_MAX_EXACT = 16
REL_MAX_DIST = 128
DIFF_QK_DIM = HEAD_DIM // 2
CONV_WIDTH = 4
GDN_CHUNK = 64
GLA_DK = HEAD_DIM // 2
GLA_GATE_RANK = 16
GLA_TAU = 16.0
GLA_CHUNK = 64
D_FF = -(-8 * D_MODEL // (3 * 256)) * 256
EPS = 1e-6

SPLIT_WIDTHS = (
    GROUP_WIDTH, GROUP_WIDTH, GROUP_WIDTH,
    GROUP_WIDTH, GROUP_WIDTH, GROUP_WIDTH,
    GROUP_WIDTH, GROUP_WIDTH, GROUP_WIDTH, GROUP_WIDTH, N_HEADS_G, N_HEADS_G,
    N_HEADS_G * GLA_DK, N_HEADS_G * GLA_DK, GROUP_WIDTH, GROUP_WIDTH, GLA_GATE_RANK,
)
D_IN_PROJ = sum(SPLIT_WIDTHS)

kernel_name = "hybrid_moba_diff_gdn_gla_trunk"


def _rmsnorm(x, g):
    xf = x.astype(jnp.float32)
    y = xf * lax.rsqrt(jnp.mean(xf * xf, axis=-1, keepdims=True) + EPS)
    return (y * g.astype(jnp.float32)).astype(x.dtype)


def _l2norm(x):
    return x * lax.rsqrt(jnp.sum(x * x, axis=-1, keepdims=True) + EPS)


def _heads(x, n):
    b, s, _ = x.shape
    return x.reshape(b, s, n, -1).transpose(0, 2, 1, 3)


def _merge(x):
    b, n, s, d = x.shape
    return x.transpose(0, 2, 1, 3).reshape(b, s, n * d)


def _rel_bucket(dist):
    n = jnp.maximum(dist, 0)
    nf = jnp.maximum(n, 1).astype(jnp.float32)
    large = REL_MAX_EXACT + (jnp.log(nf / REL_MAX_EXACT) / math.log(REL_MAX_DIST / REL_MAX_EXACT)
                             * (N_REL_BUCKETS - REL_MAX_EXACT)).astype(jnp.int32)
    large = jnp.minimum(large, N_REL_BUCKETS - 1)
    return jnp.where(n < REL_MAX_EXACT, n, large)


def _moba_attention(q, k, v, bias_tab):
    b, h, s, dh = q.shape
    nb = -(-s // MOBA_BLOCK)
    pad = nb * MOBA_BLOCK - s
    k_pad = jnp.pad(k, ((0, 0), (0, 0), (0, pad), (0, 0)))
    v_pad = jnp.pad(v, ((0, 0), (0, 0), (0, pad), (0, 0)))
    k_blk = k_pad.reshape(b, h, nb, MOBA_BLOCK, dh)
    v_blk = v_pad.reshape(b, h, nb, MOBA_BLOCK, dh)
    k_mean = jnp.mean(k_blk.astype(jnp.float32), axis=3)
    gate = jnp.einsum('bhsd,bhnd->bhsn', q.astype(jnp.float32), k_mean)
    past = jnp.arange(nb)[None, :] < (jnp.arange(s) // MOBA_BLOCK)[:, None]
    gate = jnp.where(past, gate, -jnp.inf)
    topk = min(MOBA_TOPK, nb)
    _, sel = lax.top_k(gate, topk)
    nq = s // MOBA_QCHUNK
    q_c = q.reshape(b, h, nq, MOBA_QCHUNK, dh).transpose(2, 0, 1, 3, 4)
    sel_c = sel.reshape(b, h, nq, MOBA_QCHUNK, topk).transpose(2, 0, 1, 3, 4)
    bi = jnp.arange(b)[:, None, None, None]
    hi = jnp.arange(h)[None, :, None, None]
    hi5 = jnp.arange(h)[None, :, None, None, None]
    offs = jnp.arange(MOBA_BLOCK)
    scale = dh ** -0.5
    n_sel = topk * MOBA_BLOCK

    def chunk(args):
        qc, sc, c = args
        q_pos = c * MOBA_QCHUNK + jnp.arange(MOBA_QCHUNK)
        own = (c * MOBA_QCHUNK) // MOBA_BLOCK
        kg = k_blk[bi, hi, sc]
        vg = v_blk[bi, hi, sc]
        kpos = sc[..., None] * MOBA_BLOCK + offs
        bias_sel = bias_tab[hi5, _rel_bucket(q_pos[:, None, None] - kpos)]
        s_sel = jnp.einsum('bhqd,bhqnld->bhqnl', qc, kg).astype(jnp.float32) * scale + bias_sel
        s_sel = jnp.where((sc < own)[..., None], s_sel, -jnp.inf).reshape(b, h, MOBA_QCHUNK, n_sel)
        k_own = lax.dynamic_slice_in_dim(k_pad, own * MOBA_BLOCK, MOBA_BLOCK, axis=2)
        v_own = lax.dynamic_slice_in_dim(v_pad, own * MOBA_BLOCK, MOBA_BLOCK, axis=2)
        dist = q_pos[:, None] - (own * MOBA_BLOCK + offs)[None, :]
        s_own = (jnp.einsum('bhqd,bhld->bhql', qc, k_own).astype(jnp.float32) * scale
                 + bias_tab[:, _rel_bucket(dist)])
        s_own = jnp.where(dist >= 0, s_own, -jnp.inf)
        probs = jax.nn.softmax(jnp.concatenate([s_sel, s_own], axis=-1), axis=-1).astype(v.dtype)
        p_sel = probs[..., :n_sel].reshape(b, h, MOBA_QCHUNK, topk, MOBA_BLOCK)
        return (jnp.einsum('bhqnl,bhqnld->bhqd', p_sel, vg)
                + jnp.einsum('bhql,bhld->bhqd', probs[..., n_sel:], v_own))

    out = lax.map(chunk, (q_c, sel_c, jnp.arange(nq)))
    return out.transpose(1, 2, 0, 3, 4).reshape(b, h, s, dh)


def _diff_attention(q1, q2, k1, k2, v, lam, bias_tab):
    b, h, s, dd = q1.shape
    nq = s // ATTN_QBLOCK
    kpos = jnp.arange(s)
    scale = dd ** -0.5

    def to_blocks(x):
        return x.reshape(b, h, nq, ATTN_QBLOCK, dd).transpose(2, 0, 1, 3, 4)

    def block(args):
        a1, a2, c = args
        q_pos = c * ATTN_QBLOCK + jnp.arange(ATTN_QBLOCK)
        dist = q_pos[:, None] - kpos[None, :]
        causal = dist >= 0
        bias = bias_tab[:, _rel_bucket(dist)]

        def probs(qb, kk):
            logits = jnp.einsum('bhqd,bhkd->bhqk', qb, kk).astype(jnp.float32) * scale + bias
            return jax.nn.softmax(jnp.where(causal, logits, -jnp.inf), axis=-1)

        attn = probs(a1, k1) - lam * probs(a2, k2)
        return jnp.einsum('bhqk,bhkd->bhqd', attn.astype(v.dtype), v)

    out = lax.map(block, (to_blocks(q1), to_blocks(q2), jnp.arange(nq)))
    return out.transpose(1, 2, 0, 3, 4).reshape(b, h, s, -1)


def _short_conv(x, w):
    c = x.shape[-1]
    return lax.conv_general_dilated(x, w[:, None, :].astype(x.dtype), window_strides=(1,),
                                    padding=[(CONV_WIDTH - 1, 0)],
                                    dimension_numbers=('NWC', 'WIO', 'NWC'),
                                    feature_group_count=c)


def _gated_delta_rule(q, k, v, g, beta):
    b, h, s, dk = q.shape
    dv = v.shape[-1]
    L = GDN_CHUNK
    nc = s // L
    chunk4 = lambda x: x.reshape(b, h, nc, L, x.shape[-1])
    qc = chunk4(q * dk ** -0.5)
    kc = chunk4(k)
    vc = chunk4(v)
    bc = beta.reshape(b, h, nc, L)
    gc = jnp.cumsum(g.reshape(b, h, nc, L), axis=-1)
    tril = jnp.tril(jnp.ones((L, L), dtype=bool))
    strict = jnp.tril(jnp.ones((L, L), dtype=bool), -1)
    decay = jnp.exp(jnp.where(tril, gc[..., :, None] - gc[..., None, :], -jnp.inf))
    kb = kc * bc[..., None]
    lower = jnp.where(strict, jnp.einsum('bhcid,bhcjd->bhcij', kb, kc) * decay, 0.0)
    eye = jnp.eye(L, dtype=jnp.float32)
    t_inv = lax.linalg.triangular_solve(lower + eye, jnp.broadcast_to(eye, lower.shape),
                                        left_side=True, lower=True, unit_diagonal=True)
    u = t_inv @ (vc * bc[..., None])
    w = t_inv @ (kb * jnp.exp(gc)[..., None])
    qk = jnp.where(tril, jnp.einsum('bhcid,bhcjd->bhcij', qc, kc) * decay, 0.0)
    q_dec = qc * jnp.exp(gc)[..., None]
    k_dec = kc * jnp.exp(gc[..., -1:] - gc)[..., None]
    g_last = jnp.exp(gc[..., -1])

    def step(state, xs):
        u_c, w_c, qk_c, qd_c, kd_c, gl_c = xs
        v_new = u_c - w_c @ state
        o = qd_c @ state + qk_c @ v_new
        state = state * gl_c[..., None, None] + jnp.swapaxes(kd_c, -1, -2) @ v_new
        return state, o

    xs = tuple(jnp.moveaxis(a, 2, 0) for a in (u, w, qk, q_dec, k_dec, g_last))
    _, o = lax.scan(step, jnp.zeros((b, h, dk, dv), jnp.float32), xs)
    return jnp.moveaxis(o, 0, 2).reshape(b, h, s, dv)


def _gla_chunked(q, k, v, log_a):
    b, h, s, dk = q.shape
    L = GLA_CHUNK
    nc = s // L
    tril = jnp.tril(jnp.ones((L, L), dtype=bool))

    def to_chunks(x):
        return jnp.moveaxis(x.reshape(b, h, nc, L, x.shape[-1]), 2, 0)

    qc = to_chunks(q * dk ** -0.5)
    kc = to_chunks(k)
    vc = to_chunks(v)
    bc = jnp.cumsum(to_chunks(log_a), axis=-2)

    def step(state, xs):
        q_c, k_c, v_c, b_c = xs
        rel = jnp.where(tril[:, :, None], b_c[..., :, None, :] - b_c[..., None, :, :], -jnp.inf)
        a_intra = jnp.einsum('bhid,bhjd,bhijd->bhij', q_c, k_c, jnp.exp(rel))
        b_last = b_c[..., -1:, :]
        o = (q_c * jnp.exp(b_c)) @ state + a_intra @ v_c
        state = (state * jnp.exp(b_last[..., 0, :])[..., None]
                 + jnp.swapaxes(k_c * jnp.exp(b_last - b_c), -1, -2) @ v_c)
        return state, o

    state0 = jnp.zeros((b, h, dk, v.shape[-1]), jnp.float32)
    _, o = lax.scan(step, state0, (qc, kc, vc, bc))
    return jnp.moveaxis(o, 0, 2).reshape(b, h, s, -1)


def setup_inputs(seed: int = 0) -> dict:
    key = jax.random.key(seed)
    ks = jax.random.split(key, 24)
    f32 = jnp.float32

    def nrm(k, shape, scale):
        return jax.random.normal(k, shape, f32) * scale

    def gain(k, shape):
        return 1.0 + 0.02 * jax.random.normal(k, shape, f32)

    dt = jnp.exp(jax.random.uniform(ks[9], (DEPTH, N_HEADS_G), f32, math.log(1e-3), math.log(1e-1)))
    return {
        "x": nrm(ks[0], (BATCH, SEQ, D_MODEL), 1.0),
        "p": nrm(ks[1], (DEPTH, BATCH, SEQ, PLE_DIM), 1.0),
        "norm_mix": gain(ks[2], (DEPTH, D_MODEL)),
        "w_in": nrm(ks[3], (DEPTH, D_MODEL, D_IN_PROJ), D_MODEL ** -0.5),
        "rel_bias": nrm(ks[4], (N_REL_BUCKETS, 2 * N_HEADS_G), 0.3),
        "diff_lambda": nrm(ks[5], (DEPTH, 4, DIFF_QK_DIM), 0.1),
        "diff_norm": gain(ks[6], (DEPTH, HEAD_DIM)),
        "gdn_conv": nrm(ks[7], (DEPTH, CONV_WIDTH, 3 * GROUP_WIDTH), CONV_WIDTH ** -0.5),
        "gdn_a_log": jnp.log(jax.random.uniform(ks[8], (DEPTH, N_HEADS_G), f32, 1.0, 16.0)),
        "gdn_dt_bias": dt + jnp.log(-jnp.expm1(-dt)),
        "gdn_norm": gain(ks[10], (DEPTH, HEAD_DIM)),
        "gla_w_alpha": nrm(ks[11], (DEPTH, GLA_GATE_RANK, N_HEADS_G * GLA_DK), GLA_GATE_RANK ** -0.5),
        "gla_b_alpha": nrm(ks[12], (DEPTH, N_HEADS_G * GLA_DK), 0.1),
        "gla_norm": gain(ks[13], (DEPTH, HEAD_DIM)),
        "w_out": nrm(ks[14], (DEPTH, D_MODEL, D_MODEL), D_MODEL ** -0.5),
        "norm_ffn": gain(ks[15], (DEPTH, D_MODEL)),
        "w_gate": nrm(ks[16], (DEPTH, D_MODEL, D_FF), D_MODEL ** -0.5),
        "w_up": nrm(ks[17], (DEPTH, D_MODEL, D_FF), D_MODEL ** -0.5),
        "w_down": nrm(ks[18], (DEPTH, D_FF, D_MODEL), D_FF ** -0.5),
        "norm_ple": gain(ks[19], (DEPTH, D_MODEL)),
        "w_ple_gate": nrm(ks[20], (DEPTH, D_MODEL, D_MODEL), D_MODEL ** -0.5),
        "w_ple_proj": nrm(ks[21], (DEPTH, PLE_DIM, D_MODEL), PLE_DIM ** -0.5),
        "final_norm": gain(ks[22], (D_MODEL,)),
    }


def reference(x, p, norm_mix, w_in, rel_bias, diff_lambda, diff_norm, gdn_conv, gdn_a_log,
              gdn_dt_bias, gdn_norm, gla_w_alpha, gla_b_alpha, gla_norm, w_out, norm_ffn,
              w_gate, w_up, w_down, norm_ple, w_ple_gate, w_ple_proj, final_norm):
    b, s, _ = x.shape
    hg = N_HEADS_G
    f32 = jnp.float32
    split_at = np.cumsum(SPLIT_WIDTHS)[:-1].tolist()
    bias_a = rel_bias[:, :hg].T
    bias_b = rel_bias[:, hg:].T
    h = x
    for l in range(DEPTH):
        xn = _rmsnorm(h, norm_mix[l])
        proj = xn @ w_in[l]
        (a_q, a_k, a_v, b_q, b_k, b_v, c_q, c_k, c_v, c_z, c_a, c_b,
         d_q, d_k, d_v, d_g, d_lr) = jnp.split(proj, split_at, axis=-1)

        y_a = _merge(_moba_attention(_heads(a_q, hg), _heads(a_k, hg), _heads(a_v, hg), bias_a))

        bq = b_q.reshape(b, s, hg, 2, DIFF_QK_DIM).transpose(0, 2, 3, 1, 4)
        bk = b_k.reshape(b, s, hg, 2, DIFF_QK_DIM).transpose(0, 2, 3, 1, 4)
        lam_init = 0.8 - 0.6 * math.exp(-0.3 * l)
        lq1, lk1, lq2, lk2 = diff_lambda[l].astype(f32)
        lam = jnp.exp(jnp.sum(lq1 * lk1)) - jnp.exp(jnp.sum(lq2 * lk2)) + lam_init
        o_b = _diff_attention(bq[:, :, 0], bq[:, :, 1], bk[:, :, 0], bk[:, :, 1],
                              _heads(b_v, hg), lam, bias_b)
        y_b = _merge(_rmsnorm(o_b, diff_norm[l]) * (1.0 - lam_init))

        conv = jax.nn.silu(_short_conv(jnp.concatenate([c_q, c_k, c_v], axis=-1), gdn_conv[l]))
        cq, ck, cv = jnp.split(conv, 3, axis=-1)
        cq = _l2norm(_heads(cq, hg).astype(f32))
        ck = _l2norm(_heads(ck, hg).astype(f32))
        g = -jnp.exp(gdn_a_log[l].astype(f32)) * jax.nn.softplus(c_a.astype(f32) + gdn_dt_bias[l])
        beta = jax.nn.sigmoid(c_b.astype(f32))
        o_c = _gated_delta_rule(cq, ck, _heads(cv, hg).astype(f32),
                                g.transpose(0, 2, 1), beta.transpose(0, 2, 1))
        y_c = _merge(_rmsnorm(o_c, gdn_norm[l]) * jax.nn.silu(_heads(c_z, hg)))

        log_a = jax.nn.log_sigmoid((d_lr @ gla_w_alpha[l] + gla_b_alpha[l]).astype(f32)) / GLA_TAU
        o_d = _gla_chunked(_heads(d_q, hg).astype(f32), _heads(d_k, hg).astype(f32),
                           _heads(d_v, hg).astype(f32), _heads(log_a, hg))
        y_d = _merge(_rmsnorm(o_d, gla_norm[l]) * jax.nn.silu(_heads(d_g, hg)))

        mixed = jnp.concatenate([y_a.astype(h.dtype), y_b.astype(h.dtype),
                                 y_c.astype(h.dtype), y_d.astype(h.dtype)], axis=-1)
        h = h + mixed @ w_out[l]

        hn = _rmsnorm(h, norm_ffn[l])
        h = h + (jax.nn.silu(hn @ w_gate[l]) * (hn @ w_up[l])) @ w_down[l]

        gate = jax.nn.sigmoid(_rmsnorm(h, norm_ple[l]) @ w_ple_gate[l])
        h = h + gate * (p[l] @ w_ple_proj[l])
    return _rmsnorm(h, final_norm)
```

```python
import contextlib
import math
import numpy as np
import concourse.bass as bass
import concourse.mybir as mybir
from concourse.bass_utils import run_bass_kernel_spmd

F32 = mybir.dt.float32
BF16 = mybir.dt.bfloat16
AF = mybir.ActivationFunctionType
ALU = mybir.AluOpType
AX = mybir.AxisListType

EPOCH = 30000
DMA_EPOCH = 2000
DMA_RING = 8
NSEM_POOL = 96


class Buf:
    __slots__ = ("name", "w", "r")

    def __init__(self, name=""):
        self.name = name
        self.w = None
        self.r = []


class Tile(Buf):
    __slots__ = ("t",)

    def __init__(self, t, name=""):
        Buf.__init__(self, name)
        self.t = t

    def __getitem__(self, k):
        return self.t[k]


class Prog:
    def __init__(self, nc):
        self.nc = nc
        self.stack = contextlib.ExitStack()
        self.eobj = {"pe": nc.tensor, "act": nc.scalar, "dve": nc.vector,
                     "pool": nc.gpsimd, "sp": nc.sync}
        self.esem = {}
        self.ecnt = {}
        self.eepoch = {}
        self.waited = {e: {} for e in self.eobj}
        self.sems = {}
        self.semval = {}
        self.nsem = 0
        for e in self.eobj:
            self.eepoch[e] = -1
            self._new_epoch(e)
        self.dring = []
        self.dpos = 0
        for i in range(DMA_RING):
            self.dring.append(self._new_dsem())
        self.uid = 0
        self.ninst = 0

    def _alloc_sem(self, key):
        if not hasattr(self, "sempool"):
            self.sempool = [self.stack.enter_context(self.nc.semaphore("s%d" % i)) for i in range(NSEM_POOL)]
        h = self.sempool[self.nsem]
        self.nsem += 1
        self.sems[key] = h
        self.semval[key] = 0
        return h

    def _new_epoch(self, e):
        self.eepoch[e] += 1
        key = (e, self.eepoch[e])
        self._alloc_sem(key)
        self.esem[e] = key
        self.ecnt[e] = 0

    def _new_dsem(self):
        key = ("dma", self.nsem)
        self._alloc_sem(key)
        return [key, 0]

    def _wait(self, eng, key, val):
        w = self.waited[eng]
        if w.get(key, 0) >= val:
            return
        w[key] = val
        self.eobj[eng].wait_ge(self.sems[key], val)

    def _deps(self, eng, reads, writes, is_pe=False):
        for b in reads:
            if b.w is not None:
                key, val, we = b.w
                self._wait(eng, key, val)
        for b in writes:
            if b.w is not None:
                key, val, we = b.w
                if not (is_pe and we == "pe"):
                    self._wait(eng, key, val)
            for (key, val, re_) in b.r:
                if re_ == eng:
                    continue
                self._wait(eng, key, val)

    def _mark(self, ev, reads, writes):
        for b in reads:
            b.r.append(ev)
        for b in writes:
            b.w = ev
            b.r = []

    def op(self, eng, reads, writes, fn):
        self._deps(eng, reads, writes, is_pe=(eng == "pe"))
        if self.ecnt[eng] >= EPOCH:
            self._new_epoch(eng)
        ins = fn(self.eobj[eng])
        key = self.esem[eng]
        self.ecnt[eng] += 1
        val = self.ecnt[eng]
        ins.then_inc(self.sems[key], 1)
        self.semval[key] = val
        self._mark((key, val, eng), reads, writes)
        self.ninst += 1
        return ins

    def dma(self, q, out, in_, reads, writes, **kw):
        for b in reads:
            if b.w is not None:
                self._wait(q, b.w[0], b.w[1])
        for b in writes:
            if b.w is not None:
                self._wait(q, b.w[0], b.w[1])
            for (key, val, re_) in b.r:
                self._wait(q, key, val)
        slot = self.dring[self.dpos]
        if slot[1] >= DMA_EPOCH:
            slot = self._new_dsem()
            self.dring[self.dpos] = slot
        key = slot[0]
        if slot[1] > 0:
            self._wait(q, key, 16 * slot[1])
        self.dpos = (self.dpos + 1) % DMA_RING
        ins = self.eobj[q].dma_start(out=out, in_=in_, **kw)
        slot[1] += 1
        val = 16 * slot[1]
        ins.then_inc(self.sems[key], 16)
        self.semval[key] = val
        self._mark((key, val, "dma"), reads, writes)
        self.ninst += 1
        return ins

    def barrier(self):
        for e in self.eobj:
            for key, val in list(self.semval.items()):
                if val > 0:
                    self._wait(e, key, val)

    def finish(self):
        self.barrier()

    def sbuf(self, st, shape, dtype, name=None):
        self.uid += 1
        nm = "%s_%d" % (name or "t", self.uid)
        t = st.enter_context(self.nc.sbuf_tensor(nm, list(shape), dtype))
        return Tile(t, nm)

    def psum(self, st, shape, dtype, name=None):
        self.uid += 1
        nm = "%s_%d" % (name or "p", self.uid)
        t = st.enter_context(self.nc.psum_tensor(nm, list(shape), dtype))
        return Tile(t, nm)

    def dram(self, shape, dtype, name, kind="Internal"):
        t = self.nc.dram_tensor(name, list(shape), dtype, kind=kind)
        return Tile(t.ap() if hasattr(t, "ap") else t, name)


D = 1024
S = 4096
DEPTH = 4
TB = 1024
NTB = S // TB
KC = D // 128
DFF = 2816
FC = DFF // 128
PLE = 256
EPS = 1e-6
BIG = 30000.0
NFM = 2064
NTM = 1416
NIN = NFM + NTM
FM_AQ, FM_AK, FM_BQ, FM_BK, FM_C, FM_DQ, FM_DK, FM_DLR = 0, 256, 512, 768, 1024, 1792, 1920, 2048
TM_AV, TM_BV, TM_CZ, TM_DV, TM_DG, TM_DK, TM_CA, TM_CB = 0, 256, 512, 768, 1024, 1280, 1408, 1412

_SW = (256, 256, 256, 256, 256, 256, 256, 256, 256, 256, 4, 4, 128, 128, 256, 256, 16)
_OFF = np.concatenate([[0], np.cumsum(_SW)])
(O_AQ, O_AK, O_AV, O_BQ, O_BK, O_BV, O_CQ, O_CK, O_CV, O_CZ, O_CA, O_CB, O_DQ, O_DK, O_DV,
 O_DG, O_DLR) = [int(v) for v in _OFF[:-1]]


def _col_perm():
    r = lambda o, n: list(range(o, o + n))
    fm = (r(O_AQ, 256) + r(O_AK, 256) + r(O_BQ, 256) + r(O_BK, 256) + r(O_CQ, 768)
          + r(O_DQ, 128) + r(O_DK, 128) + r(O_DLR, 16))
    tm = (r(O_AV, 256) + r(O_BV, 256) + r(O_CZ, 256) + r(O_DV, 256) + r(O_DG, 256)
          + r(O_DK, 128) + r(O_CA, 4) + r(O_CB, 4))
    assert len(fm) == NFM and len(tm) == NTM
    return np.array(fm + tm)


class K:
    pass


def build_program(n_layers=DEPTH, mixers=("a", "b", "c", "d"), debug_out=False, mixer_only=False):
    nc = bass.Bass("TRN2", target_bir_lowering=False)
    P = Prog(nc)

    def din(name, shape, dt=F32):
        return Tile(nc.dram_tensor(name, list(shape), dt, kind="ExternalInput").ap(), name)

    xT = din("xT", [D, S])
    pT = din("pT", [DEPTH, PLE, S])
    w_in = din("w_in", [DEPTH, D, NIN])
    w_out = din("w_out", [DEPTH, D, D])
    w_gate = din("w_gate", [DEPTH, D, DFF])
    w_up = din("w_up", [DEPTH, D, DFF])
    w_down = din("w_down", [DEPTH, DFF, D])
    w_pg = din("w_pg", [DEPTH, D, D])
    w_pp = din("w_pp", [DEPTH, PLE, D])
    gains = din("gains", [128, DEPTH * 3 + 1, KC])
    gtab = din("gtab", [8, 128, 1024])
    consts = din("consts", [128, 6, 128])
    ind16 = din("ind16", [16, S], BF16)
    mconst = din("mconst", [128, 3, 512])
    dlam = din("dlam", [DEPTH, 128])
    walpha = din("walpha", [DEPTH, 16, 128])
    balpha = din("balpha", [DEPTH, 128])
    glanorm = din("glanorm", [DEPTH, 64])
    gdnnorm = din("gdnnorm", [DEPTH, 64])
    hmask = din("hmask", [128, 8])
    gconv = din("gconv", [DEPTH, 128, 6, 4])
    gdtb = din("gdtb", [DEPTH, 4])
    galog = din("galog", [DEPTH, 4])
    dnorm = din("dnorm", [DEPTH, 64, 1])
    outT = Tile(nc.dram_tensor("outT", [D, S], F32, kind="ExternalOutput").ap(), "outT")

    def dscr(name, shape, dt):
        return Tile(nc.dram_tensor(name, list(shape), dt, kind="Internal").ap(), name)

    wb_in = dscr("wb_in", [DEPTH, D, NIN], BF16)
    wb_out = dscr("wb_out", [DEPTH, D, D], BF16)
    wb_gate = dscr("wb_gate", [DEPTH, D, DFF], BF16)
    wb_up = dscr("wb_up", [DEPTH, D, DFF], BF16)
    wb_down = dscr("wb_down", [DEPTH, DFF, D], BF16)
    wb_pg = dscr("wb_pg", [DEPTH, D, D], BF16)
    wb_pp = dscr("wb_pp", [DEPTH, PLE, D], BF16)
    hT = dscr("hT", [D, S], F32)
    FMS = dscr("FMS", [NFM, S], F32)
    TMS = dscr("TMS", [S, NTM], F32)
    if debug_out:
        yT = Tile(nc.dram_tensor("yT", [D, S], BF16, kind="ExternalOutput").ap(), "yT")
    else:
        yT = dscr("yT", [D, S], BF16)

    with P.stack, contextlib.ExitStack() as gst:
        cst = P.sbuf(gst, [128, 6, 128], F32, "cst")
        P.dma("sp", cst[:, :, :], consts[:, :, :], [consts], [cst])
        ident = cst.t[:, 0, :]
        ones = cst.t[:, 1, :]
        gn = P.sbuf(gst, [128, DEPTH * 3 + 1, KC], F32, "gn")
        P.dma("sp", gn[:, :, :], gains[:, :, :], [gains], [gn])
        identb = P.sbuf(gst, [128, 128], BF16, "identb")
        P.op("dve", [cst], [identb], lambda e: e.tensor_copy(out=identb[:, :], in_=ident))
        psums = [P.psum(gst, [128, 512], F32, "ps%d" % i) for i in range(7)]
        psacc = P.psum(gst, [128, 512], F32, "psacc")
        pstate = [0]

        def next_ps():
            t = psums[pstate[0] % 7]
            pstate[0] += 1
            return t

        with contextlib.ExitStack() as st:
            stg = [P.sbuf(st, [128, 2048], F32, "wstg") for _ in range(3)]
            cbf = [P.sbuf(st, [128, 2048], BF16, "wcb") for _ in range(3)]
            cnt = [0]

            def cast2d(src, dst, l, Kdim, N):
                for kc in range(Kdim // 128):
                    for c0 in range(0, N, 2048):
                        w = min(2048, N - c0)
                        i = cnt[0] % 3
                        cnt[0] += 1
                        s_, c_ = stg[i], cbf[i]
                        P.dma("sp", s_[:, 0:w], src.t[l, kc * 128:(kc + 1) * 128, c0:c0 + w], [], [s_])
                        eng = ("dve", "pool", "act")[cnt[0] % 3]
                        if eng == "act":
                            P.op("act", [s_], [c_], lambda e: e.copy(out=c_[:, 0:w], in_=s_[:, 0:w]))
                        else:
                            P.op(eng, [s_], [c_], lambda e: e.tensor_copy(out=c_[:, 0:w], in_=s_[:, 0:w]))
                        P.dma("pool", dst.t[l, kc * 128:(kc + 1) * 128, c0:c0 + w], c_[:, 0:w], [c_], [])

            for l in range(n_layers):
                cast2d(w_in, wb_in, l, D, NIN)
                if mixer_only:
                    continue
                cast2d(w_out, wb_out, l, D, D)
                cast2d(w_gate, wb_gate, l, D, DFF)
                cast2d(w_up, wb_up, l, D, DFF)
                cast2d(w_down, wb_down, l, DFF, D)
                cast2d(w_pg, wb_pg, l, D, D)
                cast2d(w_pp, wb_pp, l, PLE, D)
            P.barrier()

        kb = K()
        kb.P, kb.nc, kb.next_ps = P, nc, next_ps
        kb.ident, kb.ones, kb.identb, kb.cst, kb.gn = ident, ones, identb, cst, gn
        kb.FMS, kb.TMS, kb.yT, kb.hT, kb.gtab = FMS, TMS, yT, hT, gtab
        kb.psacc, kb.ind16, kb.mconst, kb.dlam, kb.dnorm = psacc, ind16, mconst, dlam, dnorm
        kb.walpha, kb.balpha, kb.glanorm, kb.gdnnorm, kb.hmask = walpha, balpha, glanorm, gdnnorm, hmask
        kb.gconv, kb.gdtb, kb.galog = gconv, gdtb, galog

        for l in range(n_layers + 1):
            dense_phase(kb, l, n_layers, xT, pT, outT,
                        wb_in, wb_out, wb_gate, wb_up, wb_down, wb_pg, wb_pp, have_y=bool(mixers))
            P.barrier()
            if l < n_layers:
                mixer_phase(kb, l, mixers)
                P.barrier()
            if mixer_only:
                break
        P.finish()
    return nc


def wstream(P, bufs, loads, computes):
    n = len(loads)
    nb = len(bufs)
    for i in range(min(nb, n)):
        loads[i](bufs[i % nb])
    for i in range(n):
        computes[i](bufs[i % nb])
        if i + nb < n:
            loads[i + nb](bufs[i % nb])


def dense_phase(kb, l, n_layers, xT, pT, outT, wb_in, wb_out, wb_gate, wb_up, wb_down, wb_pg, wb_pp,
                have_y=True):
    P, nc, next_ps = kb.P, kb.nc, kb.next_ps
    prev = l - 1
    with contextlib.ExitStack() as st:
        hblk = P.sbuf(st, [128, KC, TB], F32, "hblk")
        xn = P.sbuf(st, [128, KC, TB], BF16, "xn")
        wt = [P.sbuf(st, [128, KC, 512], BF16, "wt") for _ in range(2)]
        tmpf = [P.sbuf(st, [128, 512], F32, "tmpf") for _ in range(4)]
        tcnt = [0]

        def next_tmp():
            t = tmpf[tcnt[0] % 4]
            tcnt[0] += 1
            return t

        if prev >= 0:
            act = P.sbuf(st, [128, FC, TB], BF16, "act")
            wd = [P.sbuf(st, [128, FC, 256], BF16, "wd") for _ in range(2)]
            pst = P.sbuf(st, [128, TB], F32, "pst")
            pb = P.sbuf(st, [128, 2, TB], BF16, "pb")
            gt = [P.sbuf(st, [128, 512], F32, "gt") for _ in range(2)]

        def norm_block(which):
            for sb in range(TB // 512):
                sl = slice(sb * 512, (sb + 1) * 512)
                ps = next_ps()
                for kc in range(KC):
                    sq = next_tmp()
                    P.op("act", [hblk], [sq], lambda e: e.activation(out=sq[:, :], in_=hblk[:, kc, sl], func=AF.Square))
                    P.op("pe", [sq, kb.cst], [ps], lambda e: e.matmul(ps[:, :], lhsT=kb.ones, rhs=sq[:, :],
                                                                     start=(kc == 0), stop=(kc == KC - 1)))
                rstd = next_tmp()
                P.op("dve", [ps], [rstd], lambda e: e.tensor_scalar(out=rstd[:, :], in0=ps[:, :], scalar1=1.0 / D,
                                                                   scalar2=EPS, op0=ALU.mult, op1=ALU.add))
                P.op("act", [rstd], [rstd], lambda e: e.activation(out=rstd[:, :], in_=rstd[:, :], func=AF.Sqrt))
                P.op("dve", [rstd], [rstd], lambda e: e.reciprocal(out=rstd[:, :], in_=rstd[:, :]))
                for kc in range(KC):
                    P.op("dve", [hblk, rstd, kb.gn], [xn], lambda e: e.scalar_tensor_tensor(
                        out=xn[:, kc, sl], in0=hblk[:, kc, sl], scalar=kb.gn[:, which, kc:kc + 1], in1=rstd[:, :],
                        op0=ALU.mult, op1=ALU.mult))
            return rstd

        def load_w(buf, Wb, lidx, kcn, c0, w):
            P.dma("sp", buf[:, 0:kcn, 0:w],
                  Wb.t[lidx, 0:kcn * 128, c0:c0 + w].rearrange("(kc p) n -> p kc n", p=128), [], [buf])

        def fm_linear(src, kcn, Wb, lidx, N, bufs, evac, wmax=512):
            loads, comps = [], []
            for c0 in range(0, N, wmax):
                w = min(wmax, N - c0)
                loads.append(lambda buf, c0=c0, w=w: load_w(buf, Wb, lidx, kcn, c0, w))

                def comp(buf, c0=c0, w=w):
                    for m in range(0, w, 128):
                        msz = min(128, w - m)
                        for sb in range(TB // 512):
                            ps = next_ps()
                            for kc in range(kcn):
                                P.op("pe", [buf, src], [ps], lambda e: e.matmul(
                                    ps[0:msz, :], lhsT=buf[:, kc, m:m + msz], rhs=src[:, kc, sb * 512:(sb + 1) * 512],
                                    start=(kc == 0), stop=(kc == kcn - 1)))
                            evac(ps, c0 + m, msz, sb)
                comps.append(comp)
            wstream(P, bufs, loads, comps)

        for tb in range(NTB):
            t0 = tb * TB
            hsrc = xT if l <= 1 else kb.hT
            P.dma("sp", hblk[:, :, :], hsrc.t[:, t0:t0 + TB].rearrange("(kc p) t -> p kc t", p=128), [], [hblk])
            if prev >= 0:
                def ev_res(ps, n0, nsz, sb):
                    kc = n0 // 128
                    sl = slice(sb * 512, (sb + 1) * 512)
                    P.op("dve", [ps, hblk], [hblk], lambda e: e.tensor_tensor(
                        out=hblk[:, kc, sl], in0=ps[:, :], in1=hblk[:, kc, sl], op=ALU.add))
                if have_y:
                    P.dma("pool", xn[:, :, :], kb.yT.t[:, t0:t0 + TB].rearrange("(kc p) t -> p kc t", p=128), [], [xn])
                    fm_linear(xn, KC, wb_out, prev, D, wt, ev_res)
                norm_block(prev * 3 + 1)
                gate_keep = {}

                def ev_gate(ps, n0, nsz, sb):
                    g_ = gt[sb % 2]
                    P.op("act", [ps], [g_], lambda e: e.activation(out=g_[:, :], in_=ps[:, :], func=AF.Silu))
                    gate_keep[(n0, sb)] = g_

                def ev_up(ps, n0, nsz, sb):
                    g_ = gate_keep[(n0, sb)]
                    fc = n0 // 128
                    P.op("dve", [ps, g_], [act], lambda e: e.tensor_tensor(
                        out=act[:, fc, sb * 512:(sb + 1) * 512], in0=ps[:, :], in1=g_[:, :], op=ALU.mult))

                loads, comps = [], []
                for fc in range(FC):
                    for which, Wb, ev in ((0, wb_gate, ev_gate), (1, wb_up, ev_up)):
                        loads.append(lambda buf, fc=fc, Wb=Wb: load_w(buf, Wb, prev, KC, fc * 128, 128))

                        def comp(buf, fc=fc, ev=ev):
                            for sb in range(TB // 512):
                                ps = next_ps()
                                for kc in range(KC):
                                    P.op("pe", [buf, xn], [ps], lambda e: e.matmul(
                                        ps[:, :], lhsT=buf[:, kc, 0:128], rhs=xn[:, kc, sb * 512:(sb + 1) * 512],
                                        start=(kc == 0), stop=(kc == KC - 1)))
                                ev(ps, fc * 128, 128, sb)
                        comps.append(comp)
                wstream(P, wt, loads, comps)
                fm_linear(act, FC, wb_down, prev, D, wd, ev_res, wmax=256)
                norm_block(prev * 3 + 2)
                for c in range(2):
                    P.dma("pool", pst[:, :], pT.t[prev, c * 128:(c + 1) * 128, t0:t0 + TB], [], [pst])
                    P.op("pool", [pst], [pb], lambda e: e.tensor_copy(out=pb[:, c, :], in_=pst[:, :]))
                sig_keep = {}

                def ev_sig(ps, n0, nsz, sb):
                    g_ = gt[sb % 2]
                    P.op("act", [ps], [g_], lambda e: e.activation(out=g_[:, :], in_=ps[:, :], func=AF.Sigmoid))
                    sig_keep[(n0, sb)] = g_

                def ev_ple(ps, n0, nsz, sb):
                    g_ = sig_keep[(n0, sb)]
                    kc = n0 // 128
                    sl = slice(sb * 512, (sb + 1) * 512)
                    t_ = next_tmp()
                    P.op("dve", [ps, g_], [t_], lambda e: e.tensor_tensor(out=t_[:, :], in0=ps[:, :], in1=g_[:, :], op=ALU.mult))
                    P.op("pool", [t_, hblk], [hblk], lambda e: e.tensor_tensor(
                        out=hblk[:, kc, sl], in0=t_[:, :], in1=hblk[:, kc, sl], op=ALU.add))

                loads, comps = [], []
                for oc in range(KC):
                    loads.append(lambda buf, oc=oc: load_w(buf, wb_pg, prev, KC, oc * 128, 128))

                    def comp_g(buf, oc=oc):
                        for sb in range(TB // 512):
                            ps = next_ps()
                            for kc in range(KC):
                                P.op("pe", [buf, xn], [ps], lambda e: e.matmul(
                                    ps[:, :], lhsT=buf[:, kc, 0:128], rhs=xn[:, kc, sb * 512:(sb + 1) * 512],
                                    start=(kc == 0), stop=(kc == KC - 1)))
                            ev_sig(ps, oc * 128, 128, sb)
                    comps.append(comp_g)
                    loads.append(lambda buf, oc=oc: load_w(buf, wb_pp, prev, 2, oc * 128, 128))

                    def comp_p(buf, oc=oc):
                        for sb in range(TB // 512):
                            ps = next_ps()
                            for kc in range(2):
                                P.op("pe", [buf, pb], [ps], lambda e: e.matmul(
                                    ps[:, :], lhsT=buf[:, kc, 0:128], rhs=pb[:, kc, sb * 512:(sb + 1) * 512],
                                    start=(kc == 0), stop=(kc == 1)))
                            ev_ple(ps, oc * 128, 128, sb)
                    comps.append(comp_p)
                wstream(P, wt, loads, comps)

            if l == n_layers:
                for sb in range(TB // 512):
                    sl = slice(sb * 512, (sb + 1) * 512)
                    ps = next_ps()
                    for kc in range(KC):
                        sq = next_tmp()
                        P.op("act", [hblk], [sq], lambda e: e.activation(out=sq[:, :], in_=hblk[:, kc, sl], func=AF.Square))
                        P.op("pe", [sq, kb.cst], [ps], lambda e: e.matmul(ps[:, :], lhsT=kb.ones, rhs=sq[:, :],
                                                                         start=(kc == 0), stop=(kc == KC - 1)))
                    rstd = next_tmp()
                    P.op("dve", [ps], [rstd], lambda e: e.tensor_scalar(out=rstd[:, :], in0=ps[:, :], scalar1=1.0 / D,
                                                                       scalar2=EPS, op0=ALU.mult, op1=ALU.add))
                    P.op("act", [rstd], [rstd], lambda e: e.activation(out=rstd[:, :], in_=rstd[:, :], func=AF.Sqrt))
                    P.op("dve", [rstd], [rstd], lambda e: e.reciprocal(out=rstd[:, :], in_=rstd[:, :]))
                    for kc in range(KC):
                        P.op("dve", [hblk, rstd, kb.gn], [hblk], lambda e: e.scalar_tensor_tensor(
                            out=hblk[:, kc, sl], in0=hblk[:, kc, sl], scalar=kb.gn[:, DEPTH * 3, kc:kc + 1], in1=rstd[:, :],
                            op0=ALU.mult, op1=ALU.mult))
                P.dma("sp", outT.t[:, t0:t0 + TB].rearrange("(kc p) t -> p kc t", p=128), hblk[:, :, :], [hblk], [outT])
                continue

            if prev >= 0:
                P.dma("pool", kb.hT.t[:, t0:t0 + TB].rearrange("(kc p) t -> p kc t", p=128), hblk[:, :, :], [hblk], [])
            norm_block(l * 3 + 0)

            def ev_fm(ps, n0, nsz, sb):
                t_ = next_tmp()
                P.op("act", [ps], [t_], lambda e: e.copy(out=t_[0:nsz, :], in_=ps[0:nsz, :]))
                P.dma("pool", kb.FMS.t[n0:n0 + nsz, t0 + sb * 512:t0 + (sb + 1) * 512], t_[0:nsz, :], [t_], [])
            fm_linear(xn, KC, wb_in, l, NFM, wt, ev_fm)
            loads, comps = [], []
            for c0 in range(0, NTM, 512):
                w = min(512, NTM - c0)
                loads.append(lambda buf, c0=c0, w=w: load_w(buf, wb_in, l, KC, NFM + c0, w))

                def comp_tm(buf, c0=c0, w=w):
                    for tt in range(TB // 128):
                        ps = next_ps()
                        for kc in range(KC):
                            P.op("pe", [buf, xn], [ps], lambda e: e.matmul(
                                ps[:, 0:w], lhsT=xn[:, kc, tt * 128:(tt + 1) * 128], rhs=buf[:, kc, 0:w],
                                start=(kc == 0), stop=(kc == KC - 1)))
                        t_ = next_tmp()
                        P.op("dve", [ps], [t_], lambda e: e.tensor_copy(out=t_[:, 0:w], in_=ps[:, 0:w]))
                        P.dma("pool", kb.TMS.t[t0 + tt * 128:t0 + (tt + 1) * 128, c0:c0 + w], t_[:, 0:w], [t_], [])
                comps.append(comp_tm)
            wstream(P, wt, loads, comps)


def mixer_phase(kb, l, mixers):
    if "a" in mixers or "b" in mixers:
        attn_phase(kb, l, mixers)
        kb.P.barrier()
    if "d" in mixers:
        gla_phase(kb, l)
        kb.P.barrier()
    if "c" in mixers:
        gdn_phase(kb, l)
        kb.P.barrier()


def attn_phase(kb, l, mixers):
    P, nc, next_ps = kb.P, kb.nc, kb.next_ps
    lam_init = 0.8 - 0.6 * math.exp(-0.3 * l)
    with contextlib.ExitStack() as st:
        G = P.sbuf(st, [128, 1024], F32, "G")
        stg = P.sbuf(st, [64, S], F32, "stg")
        kstg = P.sbuf(st, [64, S], F32, "kstg")
        QT = P.sbuf(st, [80, S], BF16, "QT")
        KT = P.sbuf(st, [80, S], BF16, "KT")
        MT = P.sbuf(st, [16, S], BF16, "MT")
        IND = P.sbuf(st, [16, S], BF16, "IND")
        vstg = P.sbuf(st, [128, 32, 64], F32, "vstg")
        Vaug = P.sbuf(st, [128, 32, 128], BF16, "Vaug")
        PT = [P.sbuf(st, [128, 512], BF16, "PT") for _ in range(3)]
        stmp = [P.sbuf(st, [128, 512], F32, "stmp") for _ in range(2)]
        rec = P.sbuf(st, [128, 512], F32, "rec")
        on = [P.sbuf(st, [64, 512], F32, "on") for _ in range(2)]
        osb = P.sbuf(st, [64, 512], F32, "osb")
        ob = [P.sbuf(st, [64, 512], BF16, "ob") for _ in range(2)]
        sm = P.sbuf(st, [64, 16], F32, "sm")
        lamb = P.sbuf(st, [64, 128], F32, "lamb")
        lamp = P.sbuf(st, [64, 128], F32, "lamp")
        mc = P.sbuf(st, [128, 3, 512], F32, "mc")
        gm = P.sbuf(st, [128, 32, 16], F32, "gm")
        m8 = P.sbuf(st, [128, 32, 8], F32, "m8")
        sel = P.sbuf(st, [128, 32, 16], F32, "sel")
        km = P.sbuf(st, [64, 16], F32, "km")
        acc = kb.psacc
        cnt = [0]
        oncnt = [0]
        P.op("pool", [], [Vaug], lambda e: e.memset(Vaug[:, :, :], 1.0))
        P.dma("sp", IND[:, :], kb.ind16.t[:, :], [], [IND])
        P.dma("sp", mc[:, :, :], kb.mconst.t[:, :, :], [], [mc])
        if "b" in mixers:
            P.dma("sp", lamb[:, :], kb.dlam.t[l:l + 1, :].broadcast_to([64, 128]), [], [lamb])
            P.dma("sp", sm[:, 4:5], kb.dnorm.t[l, :, :], [], [sm])
            P.op("dve", [lamb], [lamp], lambda e: e.tensor_tensor(out=lamp[:, 0:32], in0=lamb[:, 0:32], in1=lamb[:, 32:64], op=ALU.mult))
            P.op("dve", [lamb, lamp], [lamp], lambda e: e.tensor_tensor(out=lamp[:, 32:64], in0=lamb[:, 64:96], in1=lamb[:, 96:128], op=ALU.mult))
            P.op("dve", [lamp], [sm], lambda e: e.tensor_reduce(out=sm[:, 0:2], in_=lamp[:, 0:64].rearrange("p (a b) -> p a b", b=32), axis=AX.X, op=ALU.add))
            P.op("act", [sm], [sm], lambda e: e.activation(out=sm[:, 0:2], in_=sm[:, 0:2], func=AF.Exp))
            P.op("dve", [sm], [sm], lambda e: e.tensor_tensor(out=sm[:, 2:3], in0=sm[:, 1:2], in1=sm[:, 0:1], op=ALU.subtract))
            P.op("dve", [sm], [sm], lambda e: e.tensor_scalar(out=sm[:, 2:3], in0=sm[:, 2:3], scalar1=-lam_init, scalar2=None, op0=ALU.add))
            P.op("dve", [sm], [sm], lambda e: e.tensor_scalar(out=sm[:, 5:6], in0=sm[:, 4:5], scalar1=(1.0 - lam_init), scalar2=None, op0=ALU.mult))

        def load_head(qoff, koff, voff, h):
            P.dma("sp", G[:, :], kb.gtab.t[h, :, :], [], [G])
            P.dma("sp", stg[:, :], kb.FMS.t[qoff:qoff + 64, :], [], [stg])
            P.dma("pool", kstg[:, :], kb.FMS.t[koff:koff + 64, :], [], [kstg])
            for q4i in range(4):
                P.dma("sp", vstg[:, q4i * 8:(q4i + 1) * 8, :], kb.TMS.t[q4i * 1024:(q4i + 1) * 1024, voff:voff + 64].rearrange("(t p) c -> p t c", p=128), [], [vstg])
            for c in range(4):
                sl = slice(c * 1024, (c + 1) * 1024)
                P.op("pool", [stg], [QT], lambda e: e.tensor_copy(out=QT[0:64, sl], in_=stg[:, sl]))
                P.op("dve", [kstg], [KT], lambda e: e.tensor_copy(out=KT[0:64, sl], in_=kstg[:, sl]))
            P.op("pool", [vstg], [Vaug], lambda e: e.tensor_copy(out=Vaug[:, :, 0:64], in_=vstg[:, :, :]))

        def moba_select():
            import os as _os
            msel = int(_os.environ.get("MOBA_SEL", 9))
            if msel == 0:
                P.op("dve", [], [MT], lambda e: e.memset(MT[:, :], 0.0))
                return
            P.op("dve", [kstg], [km], lambda e: e.tensor_reduce(out=km[:, :], in_=kstg[:, :].rearrange("p (a b) -> p a b", b=256), axis=AX.X, op=ALU.add))
            P.op("dve", [km], [km], lambda e: e.tensor_scalar(out=km[:, :], in0=km[:, :], scalar1=1.0 / 256, scalar2=None, op0=ALU.mult))
            ps = next_ps()
            for t in range(32):
                P.op("pe", [stg, km], [ps], lambda e: e.matmul(ps[:, t * 16:(t + 1) * 16], lhsT=stg[:, t * 128:(t + 1) * 128], rhs=km[:, :], start=True, stop=True))
            gmf = gm[:, :, :].rearrange("p a b -> p (a b)")
            P.op("dve", [ps, mc], [gm], lambda e: e.tensor_tensor(out=gmf, in0=ps[:, :], in1=mc[:, 0, :], op=ALU.add))
            if msel == 1:
                P.op("dve", [], [MT], lambda e: e.memset(MT[:, :], 0.0))
                return
            for t in range(32):
                P.op("dve", [gm], [m8], lambda e: e.max(out=m8[:, t, :], in_=gm[:, t, :]))
            if msel == 2:
                P.op("dve", [], [MT], lambda e: e.memset(MT[:, :], 0.0))
                return
            P.op("dve", [gm, m8], [sel], lambda e: e.tensor_tensor(out=sel[:, :, :], in0=gm[:, :, :], in1=m8[:, :, 2:3].broadcast_to([128, 32, 16]), op=ALU.is_ge))
            self_f = sel[:, :, :].rearrange("p a b -> p (a b)")
            P.op("dve", [sel, mc], [sel], lambda e: e.tensor_tensor(out=self_f, in0=self_f, in1=mc[:, 1, :], op=ALU.mult))
            P.op("dve", [sel, mc], [sel], lambda e: e.tensor_tensor(out=self_f, in0=self_f, in1=mc[:, 2, :], op=ALU.max))
            P.op("dve", [sel], [sel], lambda e: e.tensor_scalar(out=self_f, in0=self_f, scalar1=-1.0, scalar2=BIG, op0=ALU.add, op1=ALU.mult))
            if msel == 3:
                P.op("dve", [], [MT], lambda e: e.memset(MT[:, :], 0.0))
                return
            for g4 in range(8):
                ps2 = next_ps()
                for j in range(4):
                    t = g4 * 4 + j
                    P.op("pe", [sel, kb.cst], [ps2], lambda e: e.matmul(ps2[0:16, j * 128:(j + 1) * 128], lhsT=sel[:, t, :], rhs=kb.ident, start=True, stop=True))
                P.op("dve", [ps2], [MT], lambda e: e.tensor_copy(out=MT[:, g4 * 512:(g4 + 1) * 512], in_=ps2[0:16, :]))

        def attend(rows, qb, scale, masked=False):
            a = acc
            nkt = 4 * qb + 4
            for kt in range(nkt):
                sps = next_ps()
                P.op("pe", [KT, QT], [sps], lambda e: e.matmul(sps[:, :], lhsT=KT[rows, kt * 128:(kt + 1) * 128], rhs=QT[rows, qb * 512:(qb + 1) * 512], start=True, stop=not masked))
                if masked:
                    P.op("pe", [IND, MT], [sps], lambda e: e.matmul(sps[:, :], lhsT=IND[:, kt * 128:(kt + 1) * 128], rhs=MT[:, qb * 512:(qb + 1) * 512], start=False, stop=True))
                pt = PT[cnt[0] % 3]
                cnt[0] += 1
                r = kt - 4 * qb
                if r < -1:
                    P.op("act", [sps, G], [pt], lambda e: e.activation(out=pt[:, :], in_=sps[:, :], func=AF.Exp, bias=G[:, 1023:1024], scale=scale))
                else:
                    c0 = 384 - 128 * r
                    t_ = stmp[cnt[0] % 2]
                    P.op("dve", [sps, G], [t_], lambda e: e.scalar_tensor_tensor(out=t_[:, :], in0=sps[:, :], scalar=scale, in1=G[:, c0:c0 + 512], op0=ALU.mult, op1=ALU.add))
                    P.op("act", [t_], [pt], lambda e: e.activation(out=pt[:, :], in_=t_[:, :], func=AF.Exp))
                P.op("pe", [Vaug, pt], [a], lambda e: e.matmul(a[:, :], lhsT=Vaug[:, kt, :], rhs=pt[:, :], start=(kt == 0), stop=(kt == nkt - 1)))
            oncnt[0] += 1
            o_ = on[oncnt[0] % 2]
            P.op("dve", [a], [rec], lambda e: e.reciprocal(out=rec[64:128, :], in_=a[64:128, :]))
            P.op("dve", [a, rec], [o_], lambda e: e.tensor_tensor(out=o_[:, :], in0=a[0:64, :], in1=rec[64:128, :], op=ALU.mult))
            return o_

        ocnt = [0]
        if "a" in mixers:
            for h in range(4):
                load_head(FM_AQ + 64 * h, FM_AK + 64 * h, TM_AV + 64 * h, h)
                moba_select()
                for qb in range(8):
                    o_ = attend(slice(0, 64), qb, 0.125, masked=True)
                    b_ = ob[ocnt[0] % 2]
                    ocnt[0] += 1
                    P.op("act", [o_], [b_], lambda e: e.copy(out=b_[:, :], in_=o_[:, :]))
                    P.dma("pool", kb.yT.t[h * 64:(h + 1) * 64, qb * 512:(qb + 1) * 512], b_[:, :], [b_], [])
        if "b" in mixers:
            sc = 32 ** -0.5
            for h in range(4):
                load_head(FM_BQ + 64 * h, FM_BK + 64 * h, TM_BV + 64 * h, 4 + h)
                for qb in range(8):
                    o1 = attend(slice(0, 32), qb, sc)
                    o2 = attend(slice(32, 64), qb, sc)
                    P.op("dve", [o1, o2, sm], [osb], lambda e: e.scalar_tensor_tensor(out=osb[:, :], in0=o2[:, :], scalar=sm[:, 2:3], in1=o1[:, :], op0=ALU.mult, op1=ALU.add))
                    sq = stmp[0]
                    P.op("act", [osb], [sq], lambda e: e.activation(out=sq[0:64, :], in_=osb[:, :], func=AF.Square))
                    ps = next_ps()
                    P.op("pe", [sq, kb.cst], [ps], lambda e: e.matmul(ps[0:64, :], lhsT=kb.ones[0:64, 0:64], rhs=sq[0:64, :], start=True, stop=True))
                    r_ = stmp[1]
                    P.op("dve", [ps], [r_], lambda e: e.tensor_scalar(out=r_[0:64, :], in0=ps[0:64, :], scalar1=1.0 / 64, scalar2=EPS, op0=ALU.mult, op1=ALU.add))
                    P.op("act", [r_], [r_], lambda e: e.activation(out=r_[0:64, :], in_=r_[0:64, :], func=AF.Sqrt))
                    P.op("dve", [r_], [r_], lambda e: e.reciprocal(out=r_[0:64, :], in_=r_[0:64, :]))
                    b_ = ob[ocnt[0] % 2]
                    ocnt[0] += 1
                    P.op("dve", [osb, r_, sm], [b_], lambda e: e.scalar_tensor_tensor(out=b_[:, :], in0=osb[:, :], scalar=sm[:, 5:6], in1=r_[0:64, :], op0=ALU.mult, op1=ALU.mult))
                    P.dma("pool", kb.yT.t[256 + h * 64:256 + (h + 1) * 64, qb * 512:(qb + 1) * 512], b_[:, :], [b_], [])


def norm_gate_out(kb, P, next_ps, o_ps, gsil, osb, sq, st4, y, ybf, row0, c):
    P.op("act", [o_ps], [osb], lambda e: e.copy(out=osb[:, :], in_=o_ps[:, 0:256]))
    P.op("act", [osb], [sq], lambda e: e.activation(out=sq[:, :], in_=osb[:, :], func=AF.Square))
    P.op("dve", [sq], [st4], lambda e: e.tensor_reduce(out=st4[:, 0:4], in_=sq[:, :].rearrange("p (h d) -> p h d", d=64), axis=AX.X, op=ALU.add))
    P.op("dve", [st4], [st4], lambda e: e.tensor_scalar(out=st4[:, 0:4], in0=st4[:, 0:4], scalar1=1.0 / 64, scalar2=EPS, op0=ALU.mult, op1=ALU.add))
    P.op("act", [st4], [st4], lambda e: e.activation(out=st4[:, 0:4], in_=st4[:, 0:4], func=AF.Sqrt))
    P.op("dve", [st4], [st4], lambda e: e.reciprocal(out=st4[:, 0:4], in_=st4[:, 0:4]))
    for h in range(4):
        hs = slice(h * 64, (h + 1) * 64)
        P.op("dve", [osb, st4, gsil], [y], lambda e: e.scalar_tensor_tensor(out=y[:, hs], in0=osb[:, hs], scalar=st4[:, h:h + 1], in1=gsil[:, hs], op0=ALU.mult, op1=ALU.mult))
    pt = next_ps()
    for j in range(2):
        P.op("pe", [y, kb.cst], [pt], lambda e: e.transpose(pt[:, j * 128:(j + 1) * 128], y[:, j * 128:(j + 1) * 128], kb.ident))
    P.op("act", [pt], [ybf], lambda e: e.copy(out=ybf[:, :], in_=pt[:, 0:256]))
    for j in range(2):
        P.dma("pool", kb.yT.t[row0 + j * 128:row0 + (j + 1) * 128, c * 128:(c + 1) * 128], ybf[:, j * 128:(j + 1) * 128], [ybf], [])


def gla_phase(kb, l):
    P, nc, next_ps = kb.P, kb.nc, kb.next_ps
    U = kb.cst.t[:, 2, :]
    with contextlib.ExitStack() as st:
        qT = P.sbuf(st, [128, S], F32, "gqT")
        kT = P.sbuf(st, [128, S], F32, "gkT")
        lrT = P.sbuf(st, [17, S], F32, "glr")
        waug = P.sbuf(st, [17, 128], F32, "waug")
        Uneg = P.sbuf(st, [128, 128], F32, "Uneg")
        Aneg = P.sbuf(st, [128, 128], F32, "Aneg")
        gainb = P.sbuf(st, [128, 256], F32, "gainb")
        hm = P.sbuf(st, [128, 8], F32, "hm")
        Sf = P.sbuf(st, [128, 256], F32, "Sf")
        Sb = P.sbuf(st, [128, 256], BF16, "Sb")
        nb = 2
        kk = [P.sbuf(st, [128, 128], F32, "kk") for _ in range(nb)]
        vv = [P.sbuf(st, [128, 256], F32, "vv") for _ in range(nb)]
        gg = [P.sbuf(st, [128, 256], F32, "gg") for _ in range(nb)]
        vb = [P.sbuf(st, [128, 256], BF16, "vb") for _ in range(nb)]
        lnr = [P.sbuf(st, [128, 128], F32, "lnr") for _ in range(nb)]
        eb = [P.sbuf(st, [128, 128], F32, "eb") for _ in range(nb)]
        enb = [P.sbuf(st, [128, 128], F32, "enb") for _ in range(nb)]
        bsb = [P.sbuf(st, [128, 128], F32, "bsb") for _ in range(nb)]
        qsm = [P.sbuf(st, [128, 4, 128], BF16, "qsm") for _ in range(nb)]
        ksT = [P.sbuf(st, [128, 128], BF16, "ksT") for _ in range(nb)]
        kdec = [P.sbuf(st, [128, 128], BF16, "kdec") for _ in range(nb)]
        Am = [P.sbuf(st, [128, 4, 128], BF16, "Am") for _ in range(nb)]
        gl = [P.sbuf(st, [128, 1], F32, "gl") for _ in range(nb)]
        gsil = [P.sbuf(st, [128, 256], F32, "gsil") for _ in range(nb)]
        osb = [P.sbuf(st, [128, 256], F32, "osb") for _ in range(nb)]
        sq = [P.sbuf(st, [128, 256], F32, "sq") for _ in range(nb)]
        st4 = [P.sbuf(st, [128, 4], F32, "st4") for _ in range(nb)]
        y = [P.sbuf(st, [128, 256], F32, "y") for _ in range(nb)]
        ybf = [P.sbuf(st, [128, 256], BF16, "ybf") for _ in range(nb)]

        P.dma("sp", qT[:, :], kb.FMS.t[FM_DQ:FM_DQ + 128, :], [], [qT])
        P.dma("pool", kT[:, :], kb.FMS.t[FM_DK:FM_DK + 128, :], [], [kT])
        P.op("dve", [], [lrT], lambda e: e.memset(lrT[:, :], 1.0))
        P.dma("sp", lrT[0:16, :], kb.FMS.t[FM_DLR:FM_DLR + 16, :], [], [lrT])
        P.dma("sp", waug[0:16, :], kb.walpha.t[l, :, :], [], [waug])
        P.dma("sp", waug[16:17, :], kb.balpha.t[l:l + 1, :], [], [waug])
        for h in range(4):
            P.dma("sp", gainb[:, h * 64:(h + 1) * 64], kb.glanorm.t[l:l + 1, :].broadcast_to([128, 64]), [], [gainb])
        P.dma("sp", hm[:, :], kb.hmask.t[:, :], [], [hm])
        P.op("dve", [kb.cst], [Uneg], lambda e: e.tensor_scalar(out=Uneg[:, :], in0=U, scalar1=-1.0 / 16, scalar2=None, op0=ALU.mult))
        P.op("dve", [kb.cst], [Aneg], lambda e: e.tensor_scalar(out=Aneg[:, :], in0=kb.ones, scalar1=-1.0 / 16, scalar2=None, op0=ALU.mult))
        P.op("dve", [], [Sf], lambda e: e.memset(Sf[:, :], 0.0))
        P.op("pool", [], [Sb], lambda e: e.memset(Sb[:, :], 0.0))

        for c in range(S // 128):
            i = c % nb
            sl = slice(c * 128, (c + 1) * 128)
            kk_, vv_, gg_, vb_, lnr_, eb_, enb_, bsb_ = kk[i], vv[i], gg[i], vb[i], lnr[i], eb[i], enb[i], bsb[i]
            qsm_, ksT_, kdec_, Am_, gl_, gsil_ = qsm[i], ksT[i], kdec[i], Am[i], gl[i], gsil[i]
            P.dma("sp", kk_[:, :], kb.TMS.t[sl, TM_DK:TM_DK + 128], [], [kk_])
            P.dma("sp", vv_[:, :], kb.TMS.t[sl, TM_DV:TM_DV + 256], [], [vv_])
            P.dma("sp", gg_[:, :], kb.TMS.t[sl, TM_DG:TM_DG + 256], [], [gg_])
            P.op("pool", [vv_], [vb_], lambda e: e.tensor_copy(out=vb_[:, :], in_=vv_[:, :]))
            P.op("act", [gg_], [gsil_], lambda e: e.activation(out=gsil_[:, :], in_=gg_[:, :], func=AF.Silu))
            P.op("pool", [gsil_, gainb], [gsil_], lambda e: e.tensor_tensor(out=gsil_[:, :], in0=gsil_[:, :], in1=gainb[:, :], op=ALU.mult))
            pre = next_ps()
            P.op("pe", [lrT, waug], [pre], lambda e: e.matmul(pre[:, 0:128], lhsT=lrT[0:17, sl], rhs=waug[0:17, :], start=True, stop=True))
            P.op("act", [pre], [lnr_], lambda e: e.activation(out=lnr_[:, :], in_=pre[:, 0:128], func=AF.Exp, scale=-1.0))
            P.op("act", [lnr_], [lnr_], lambda e: e.activation(out=lnr_[:, :], in_=lnr_[:, :], func=AF.Ln, bias=1.0))
            pb_ = next_ps()
            P.op("pe", [Uneg, lnr_], [pb_], lambda e: e.matmul(pb_[:, 0:128], lhsT=Uneg[:, :], rhs=lnr_[:, :], start=True, stop=True))
            P.op("pe", [Aneg, lnr_], [pb_], lambda e: e.matmul(pb_[:, 128:256], lhsT=Aneg[:, :], rhs=lnr_[:, :], start=True, stop=True))
            P.op("pe", [Uneg, lnr_], [pb_], lambda e: e.matmul(pb_[:, 256:384], lhsT=lnr_[:, :], rhs=Uneg[:, :], start=True, stop=True))
            P.op("pe", [Aneg, lnr_], [pb_], lambda e: e.matmul(pb_[:, 384:385], lhsT=lnr_[:, :], rhs=Aneg[:, 0:1], start=True, stop=True))
            P.op("act", [pb_], [eb_], lambda e: e.activation(out=eb_[:, :], in_=pb_[:, 256:384], func=AF.Exp))
            P.op("act", [pb_], [enb_], lambda e: e.activation(out=enb_[:, :], in_=pb_[:, 256:384], func=AF.Exp, scale=-1.0))
            P.op("act", [pb_], [gl_], lambda e: e.activation(out=gl_[:, :], in_=pb_[:, 384:385], func=AF.Exp))
            P.op("act", [pb_], [bsb_], lambda e: e.copy(out=bsb_[:, :], in_=pb_[:, 0:128]))
            P.op("dve", [pb_, bsb_], [bsb_], lambda e: e.tensor_tensor(out=bsb_[:, :], in0=pb_[:, 128:256], in1=bsb_[:, :], op=ALU.subtract))
            P.op("act", [bsb_], [bsb_], lambda e: e.activation(out=bsb_[:, :], in_=bsb_[:, :], func=AF.Exp))
            P.op("dve", [kk_, bsb_], [kdec_], lambda e: e.tensor_tensor(out=kdec_[:, :], in0=kk_[:, :], in1=bsb_[:, :], op=ALU.mult))
            for h in range(4):
                P.op("dve", [qT, hm, eb_], [qsm_], lambda e: e.scalar_tensor_tensor(out=qsm_[:, h, :], in0=qT[:, sl], scalar=hm[:, h:h + 1], in1=eb_[:, :], op0=ALU.mult, op1=ALU.mult))
            P.op("dve", [kT, enb_], [ksT_], lambda e: e.tensor_tensor(out=ksT_[:, :], in0=kT[:, sl], in1=enb_[:, :], op=ALU.mult))
            pa = next_ps()
            for h in range(4):
                P.op("pe", [ksT_, qsm_], [pa], lambda e: e.matmul(pa[:, h * 128:(h + 1) * 128], lhsT=ksT_[:, :], rhs=qsm_[:, h, :], start=True, stop=True))
            for h in range(4):
                eng = "dve" if h % 2 == 0 else "pool"
                if eng == "pool":
                    eng = "dve"
                P.op(eng, [pa, kb.cst], [Am_], lambda e: e.tensor_tensor(out=Am_[:, h, :], in0=pa[:, h * 128:(h + 1) * 128], in1=U, op=ALU.mult))
            po = next_ps()
            for h in range(4):
                hs = slice(h * 64, (h + 1) * 64)
                P.op("pe", [qsm_, Sb], [po], lambda e: e.matmul(po[:, hs], lhsT=qsm_[:, h, :], rhs=Sb[:, hs], start=True, stop=False))
                P.op("pe", [Am_, vb_], [po], lambda e: e.matmul(po[:, hs], lhsT=Am_[:, h, :], rhs=vb_[:, hs], start=False, stop=True))
            pd = next_ps()
            P.op("pe", [kdec_, vb_], [pd], lambda e: e.matmul(pd[:, 0:256], lhsT=kdec_[:, :], rhs=vb_[:, :], start=True, stop=True))
            P.op("dve", [Sf, gl_, pd], [Sf], lambda e: e.scalar_tensor_tensor(out=Sf[:, :], in0=Sf[:, :], scalar=gl_[:, 0:1], in1=pd[:, 0:256], op0=ALU.mult, op1=ALU.add))
            P.op("act", [Sf], [Sb], lambda e: e.copy(out=Sb[:, :], in_=Sf[:, :]))
            norm_gate_out(kb, P, next_ps, po, gsil_, osb[i], sq[i], st4[i], y[i], ybf[i], 768, c)


def gdn_phase(kb, l):
    P, nc, next_ps = kb.P, kb.nc, kb.next_ps
    ident, ones = kb.ident, kb.ones
    U = kb.cst.t[:, 2, :]
    SU = kb.cst.t[:, 3, :]
    SL = kb.cst.t[:, 4, :]
    BD = kb.cst.t[:, 5, :]
    import os as _os
    NCH = int(_os.environ.get("GDN_NCH", S // 128))
    GST = int(_os.environ.get("GDN_STAGE", 9))
    with contextlib.ExitStack() as st:
        qTn = P.sbuf(st, [128, 2, S], F32, "cqT")
        kTn = P.sbuf(st, [128, 2, S], F32, "ckT")
        vTc = P.sbuf(st, [128, 2, S], F32, "cvT")
        cw = P.sbuf(st, [128, 6, 4], F32, "cw")
        P.dma("sp", cw[:, :, :], kb.gconv.t[l, :, :, :], [], [cw])
        with contextlib.ExitStack() as st0:
            xp = P.sbuf(st0, [128, S + 3], F32, "xp")
            accs = [P.sbuf(st0, [128, S], F32, "cacc") for _ in range(2)]
            sqt = [P.sbuf(st0, [128, 512], F32, "csq") for _ in range(2)]
            rst = [P.sbuf(st0, [128, 512], F32, "crs") for _ in range(2)]
            P.op("pool", [], [xp], lambda e: e.memset(xp[:, 0:3], 0.0))
            for j in range(6):
                ac = accs[j % 2]
                P.dma("sp", xp[:, 3:S + 3], kb.FMS.t[FM_C + j * 128:FM_C + (j + 1) * 128, :], [], [xp])
                P.op("dve", [xp, cw], [ac], lambda e: e.tensor_scalar(out=ac[:, :], in0=xp[:, 0:S], scalar1=cw[:, j, 0:1], scalar2=None, op0=ALU.mult))
                for i in range(1, 4):
                    P.op("dve", [xp, cw, ac], [ac], lambda e: e.scalar_tensor_tensor(out=ac[:, :], in0=xp[:, i:S + i], scalar=cw[:, j, i:i + 1], in1=ac[:, :], op0=ALU.mult, op1=ALU.add))
                dst = (qTn, kTn, vTc)[j // 2]
                jj = j % 2
                if j >= 4:
                    P.op("act", [ac], [dst], lambda e: e.activation(out=dst[:, jj, :], in_=ac[:, :], func=AF.Silu))
                    continue
                P.op("act", [ac], [ac], lambda e: e.activation(out=ac[:, :], in_=ac[:, :], func=AF.Silu))
                qscale = 0.125 if j < 2 else 1.0
                for sb in range(S // 512):
                    sl = slice(sb * 512, (sb + 1) * 512)
                    sq_, rs_ = sqt[sb % 2], rst[sb % 2]
                    P.op("act", [ac], [sq_], lambda e: e.activation(out=sq_[:, :], in_=ac[:, sl], func=AF.Square))
                    ps = next_ps()
                    P.op("pe", [sq_, kb.cst], [ps], lambda e: e.matmul(ps[:, :], lhsT=BD, rhs=sq_[:, :], start=True, stop=True))
                    P.op("dve", [ps], [rs_], lambda e: e.tensor_scalar(out=rs_[:, :], in0=ps[:, :], scalar1=EPS, scalar2=None, op0=ALU.add))
                    P.op("act", [rs_], [rs_], lambda e: e.activation(out=rs_[:, :], in_=rs_[:, :], func=AF.Sqrt))
                    P.op("dve", [rs_], [rs_], lambda e: e.reciprocal(out=rs_[:, :], in_=rs_[:, :]))
                    P.op("dve", [ac, rs_], [dst], lambda e: e.scalar_tensor_tensor(out=dst[:, jj, sl], in0=ac[:, sl], scalar=qscale, in1=rs_[:, :], op0=ALU.mult, op1=ALU.mult))
            P.barrier()
        if GST == 0:
            return
        c128 = lambda nm, dt=F32: P.sbuf(st, [128, 4, 128], dt, nm)
        I4, SL4 = c128("I4"), c128("SL4")
        NEGU = P.sbuf(st, [128, 128], F32, "NEGU")
        NEGL = P.sbuf(st, [128, 128], F32, "NEGL")
        for h in range(4):
            P.op("pool", [kb.cst], [I4], lambda e: e.tensor_copy(out=I4[:, h, :], in_=ident))
            P.op("pool", [kb.cst], [SL4], lambda e: e.tensor_copy(out=SL4[:, h, :], in_=SL))
        P.op("dve", [kb.cst], [NEGU], lambda e: e.tensor_scalar(out=NEGU[:, :], in0=SU, scalar1=-BIG, scalar2=None, op0=ALU.mult))
        P.op("dve", [kb.cst], [NEGL], lambda e: e.tensor_scalar(out=NEGL[:, :], in0=SL, scalar1=-BIG, scalar2=None, op0=ALU.mult))
        gainb = P.sbuf(st, [128, 256], F32, "cgainb")
        for h in range(4):
            P.dma("sp", gainb[:, h * 64:(h + 1) * 64], kb.gdnnorm.t[l:l + 1, :].broadcast_to([128, 64]), [], [gainb])
        ab = P.sbuf(st, [128, 8], F32, "cab")
        P.dma("sp", ab[:, 0:4], kb.gdtb.t[l:l + 1, :].broadcast_to([128, 4]), [], [ab])
        P.dma("sp", ab[:, 4:8], kb.galog.t[l:l + 1, :].broadcast_to([128, 4]), [], [ab])
        P.op("act", [ab], [ab], lambda e: e.activation(out=ab[:, 4:8], in_=ab[:, 4:8], func=AF.Exp))
        P.op("dve", [ab], [ab], lambda e: e.tensor_scalar(out=ab[:, 4:8], in0=ab[:, 4:8], scalar1=-1.0, scalar2=None, op0=ALU.mult))
        S4 = P.sbuf(st, [64, 4, 64], F32, "S4")
        P.op("dve", [], [S4], lambda e: e.memset(S4[:, :, :], 0.0))
        nb = 2
        mk = lambda shape, nm, dt=F32: [P.sbuf(st, shape, dt, nm) for _ in range(nb)]
        cz, cabt, sc8 = mk([128, 256], "cz"), mk([128, 8], "cabt"), mk([128, 24], "sc8")
        kt, vt, kbt, vbt, kbg, kdec = (mk([128, 256], n) for n in ("kt", "vt", "kbt", "vbt", "kbg", "kdec"))
        q4, k4 = mk([64, 4, 128], "q4"), mk([64, 4, 128], "k4")
        Ug4, nUg4, D4, DT4, N4, M4, P4, qkT4 = (mk([128, 4, 128], n) for n in ("Ug4", "nUg4", "D4", "DT4", "N4", "M4", "P4", "qkT4"))
        usb, vnew, o2sb, osb = (mk([128, 256], n) for n in ("usb", "vnew", "o2sb", "cosb"))
        wT4 = mk([64, 4, 128], "wT4")
        gsil, osb2, sq, y = (mk([128, 256], n) for n in ("cgsil", "cosb2", "csq2", "cy"))
        st4 = mk([128, 4], "cst4")
        ybf = mk([128, 256], "cybf", BF16)

        try:
            print("GDN sbuf remaining", nc.sbuf_bytes_remaining, flush=True)
        except Exception as ex:
            print("sbuf_bytes_remaining failed", ex)

        def bc(ap4):
            return ap4.unsqueeze(2).broadcast_to([128, 4, 64])

        v3 = lambda t: t[:, :].rearrange("p (h d) -> p h d", d=64)

        for c in range(NCH):
            i = c % nb
            sl = slice(c * 128, (c + 1) * 128)
            cz_, cab_, sc_ = cz[i], cabt[i], sc8[i]
            kt_, vt_, kbt_, vbt_, kbg_, kdec_ = kt[i], vt[i], kbt[i], vbt[i], kbg[i], kdec[i]
            q4_, k4_, Ug_, nUg_, D_, DT_, N_, M_, P_, qk_ = q4[i], k4[i], Ug4[i], nUg4[i], D4[i], DT4[i], N4[i], M4[i], P4[i], qkT4[i]
            usb_, vnew_, o2sb_, osb_, wT_, gsil_ = usb[i], vnew[i], o2sb[i], osb[i], wT4[i], gsil[i]
            P.dma("sp", cz_[:, :], kb.TMS.t[sl, TM_CZ:TM_CZ + 256], [], [cz_])
            P.dma("sp", cab_[:, :], kb.TMS.t[sl, TM_CA:TM_CA + 8], [], [cab_])
            P.op("act", [cz_], [gsil_], lambda e: e.activation(out=gsil_[:, :], in_=cz_[:, :], func=AF.Silu))
            P.op("pool", [gsil_, gainb], [gsil_], lambda e: e.tensor_tensor(out=gsil_[:, :], in0=gsil_[:, :], in1=gainb[:, :], op=ALU.mult))
            P.op("dve", [cab_, ab], [sc_], lambda e: e.tensor_tensor(out=sc_[:, 0:4], in0=cab_[:, 0:4], in1=ab[:, 0:4], op=ALU.add))
            P.op("act", [sc_], [sc_], lambda e: e.activation(out=sc_[:, 0:4], in_=sc_[:, 0:4], func=AF.Exp))
            P.op("act", [sc_], [sc_], lambda e: e.activation(out=sc_[:, 0:4], in_=sc_[:, 0:4], func=AF.Ln, bias=1.0))
            P.op("dve", [sc_, ab], [sc_], lambda e: e.tensor_tensor(out=sc_[:, 0:4], in0=sc_[:, 0:4], in1=ab[:, 4:8], op=ALU.mult))
            P.op("act", [cab_], [sc_], lambda e: e.activation(out=sc_[:, 4:8], in_=cab_[:, 4:8], func=AF.Sigmoid))
            pg = next_ps()
            P.op("pe", [sc_, kb.cst], [pg], lambda e: e.matmul(pg[:, 0:4], lhsT=U, rhs=sc_[:, 0:4], start=True, stop=True))
            P.op("pe", [sc_, kb.cst], [pg], lambda e: e.matmul(pg[:, 4:8], lhsT=SL, rhs=sc_[:, 0:4], start=True, stop=True))
            P.op("pe", [sc_, kb.cst], [pg], lambda e: e.matmul(pg[:, 8:12], lhsT=ones, rhs=sc_[:, 0:4], start=True, stop=True))
            P.op("act", [pg], [sc_], lambda e: e.activation(out=sc_[:, 8:20], in_=pg[:, 0:12], func=AF.Exp))
            if GST == 1:
                continue
            ptk = next_ps()
            for j in range(2):
                P.op("pe", [kTn, kb.cst], [ptk], lambda e: e.transpose(ptk[:, j * 128:(j + 1) * 128], kTn[:, j, sl], ident))
                P.op("pe", [vTc, kb.cst], [ptk], lambda e: e.transpose(ptk[:, 256 + j * 128:256 + (j + 1) * 128], vTc[:, j, sl], ident))
            P.op("act", [ptk], [kt_], lambda e: e.copy(out=kt_[:, :], in_=ptk[:, 0:256]))
            P.op("dve", [ptk], [vt_], lambda e: e.tensor_copy(out=vt_[:, :], in_=ptk[:, 256:512]))
            for h in range(4):
                pb, j = 64 * (h % 2), h // 2
                P.op("dve", [qTn], [q4_], lambda e: e.tensor_copy(out=q4_[:, h, :], in_=qTn[pb:pb + 64, j, sl]))
                P.op("dve", [kTn], [k4_], lambda e: e.tensor_copy(out=k4_[:, h, :], in_=kTn[pb:pb + 64, j, sl]))
            P.op("dve", [kt_, sc_], [kbt_], lambda e: e.tensor_tensor(out=v3(kbt_), in0=v3(kt_), in1=bc(sc_[:, 4:8]), op=ALU.mult))
            P.op("dve", [vt_, sc_], [vbt_], lambda e: e.tensor_tensor(out=v3(vbt_), in0=v3(vt_), in1=bc(sc_[:, 4:8]), op=ALU.mult))
            P.op("dve", [kbt_, sc_], [kbg_], lambda e: e.tensor_tensor(out=v3(kbg_), in0=v3(kbt_), in1=bc(sc_[:, 8:12]), op=ALU.mult))
            P.op("dve", [kt_, sc_], [kdec_], lambda e: e.tensor_tensor(out=v3(kdec_), in0=v3(kt_), in1=bc(sc_[:, 12:16]), op=ALU.mult))
            for h in range(4):
                P.op("dve", [sc_, kb.cst], [Ug_], lambda e: e.tensor_scalar(out=Ug_[:, h, :], in0=U, scalar1=sc_[:, h:h + 1], scalar2=None, op0=ALU.mult))
            P.op("dve", [Ug_], [nUg_], lambda e: e.tensor_scalar(out=nUg_[:, :, :], in0=Ug_[:, :, :], scalar1=-1.0, scalar2=None, op0=ALU.mult))
            if GST == 2:
                continue
            pa, pat = next_ps(), next_ps()
            for h in range(4):
                hs = slice(h * 128, (h + 1) * 128)
                P.op("pe", [Ug_, kb.cst], [pa], lambda e: e.matmul(pa[:, hs], lhsT=Ug_[:, h, :], rhs=ones, start=True, stop=False))
                P.op("pe", [nUg_, kb.cst], [pa], lambda e: e.matmul(pa[:, hs], lhsT=ones, rhs=nUg_[:, h, :], start=False, stop=False))
                P.op("pe", [NEGU, kb.cst], [pa], lambda e: e.matmul(pa[:, hs], lhsT=ident, rhs=NEGU[:, :], start=False, stop=True))
                P.op("pe", [Ug_, kb.cst], [pat], lambda e: e.matmul(pat[:, hs], lhsT=ones, rhs=Ug_[:, h, :], start=True, stop=False))
                P.op("pe", [nUg_, kb.cst], [pat], lambda e: e.matmul(pat[:, hs], lhsT=nUg_[:, h, :], rhs=ones, start=False, stop=False))
                P.op("pe", [NEGL, kb.cst], [pat], lambda e: e.matmul(pat[:, hs], lhsT=ident, rhs=NEGL[:, :], start=False, stop=True))
            f4 = lambda t: t[:, :, :].rearrange("p h n -> p (h n)")
            P.op("act", [pa], [D_], lambda e: e.activation(out=f4(D_), in_=pa[:, :], func=AF.Exp))
            P.op("act", [pat], [DT_], lambda e: e.activation(out=f4(DT_), in_=pat[:, :], func=AF.Exp))
            if GST == 3:
                continue
            pgm, pqk = next_ps(), next_ps()
            for h in range(4):
                hs = slice(h * 128, (h + 1) * 128)
                P.op("pe", [k4_], [pgm], lambda e: e.matmul(pgm[:, hs], lhsT=k4_[:, h, :], rhs=k4_[:, h, :], start=True, stop=True))
                P.op("pe", [k4_, q4_], [pqk], lambda e: e.matmul(pqk[:, hs], lhsT=k4_[:, h, :], rhs=q4_[:, h, :], start=True, stop=True))
            SUB = int(_os.environ.get("GDN_SUB", 9))
            if SUB == 0:
                continue
            for h in range(4):
                hs = slice(h * 128, (h + 1) * 128)
                P.op("dve", [pgm, sc_, D_], [N_], lambda e: e.scalar_tensor_tensor(out=N_[:, h, :], in0=pgm[:, hs], scalar=sc_[:, 4 + h:5 + h], in1=D_[:, h, :], op0=ALU.mult, op1=ALU.mult))
            if SUB == 1:
                continue
            P.op("dve", [N_, SL4], [N_], lambda e: e.tensor_tensor(out=f4(N_), in0=f4(N_), in1=f4(SL4), op=ALU.mult))
            P.op("dve", [pqk, DT_], [qk_], lambda e: e.tensor_tensor(out=f4(qk_), in0=pqk[:, :], in1=f4(DT_), op=ALU.mult))
            if SUB == 2:
                continue
            pm = next_ps()
            for h in range(4):
                P.op("pe", [N_, kb.cst], [pm], lambda e: e.transpose(pm[:, h * 128:(h + 1) * 128], N_[:, h, :], ident))
            if SUB == 3:
                continue
            P.op("act", [pm], [M_], lambda e: e.copy(out=f4(M_), in_=pm[:, :]))
            if SUB == 4:
                continue
            PV = int(_os.environ.get("GDN_PV", 3))
            if PV == 1:
                P.op("dve", [pm, I4], [P_], lambda e: e.scalar_tensor_tensor(out=f4(P_), in0=pm[:, :], scalar=-1.0, in1=f4(I4), op0=ALU.mult, op1=ALU.add))
            elif PV == 2:
                P.op("act", [pm], [P_], lambda e: e.mul(out=f4(P_), in_=pm[:, :], mul=-1.0))
                P.op("pool", [P_, I4], [P_], lambda e: e.tensor_tensor(out=f4(P_), in0=f4(P_), in1=f4(I4), op=ALU.add))
            elif PV == 3:
                P.op("dve", [M_, I4], [P_], lambda e: e.tensor_tensor(out=f4(P_), in0=f4(I4), in1=f4(M_), op=ALU.subtract))
            if GST == 4:
                continue
            for r in range(6):
                last = (r == 5)
                pn = next_ps()
                for h in range(4):
                    P.op("pe", [M_, N_], [pn], lambda e: e.matmul(pn[:, h * 128:(h + 1) * 128], lhsT=M_[:, h, :], rhs=N_[:, h, :], start=True, stop=True))
                if not last:
                    pm2 = next_ps()
                    for h in range(4):
                        P.op("pe", [M_, N_], [pm2], lambda e: e.matmul(pm2[:, h * 128:(h + 1) * 128], lhsT=N_[:, h, :], rhs=M_[:, h, :], start=True, stop=True))
                P.op("dve", [pn], [N_], lambda e: e.tensor_copy(out=f4(N_), in_=pn[:, :]))
                if not last:
                    P.op("act", [pm2], [M_], lambda e: e.copy(out=f4(M_), in_=pm2[:, :]))
                pp = next_ps()
                for h in range(4):
                    P.op("pe", [N_, P_], [pp], lambda e: e.matmul(pp[:, h * 128:(h + 1) * 128], lhsT=N_[:, h, :], rhs=P_[:, h, :], start=True, stop=True))
                P.op("dve", [pp, P_], [P_], lambda e: e.tensor_tensor(out=f4(P_), in0=pp[:, :], in1=f4(P_), op=ALU.add))
            if GST == 5:
                continue
            pu, pw = next_ps(), next_ps()
            for h in range(4):
                P.op("pe", [P_, vbt_], [pu], lambda e: e.matmul(pu[:, h * 64:(h + 1) * 64], lhsT=P_[:, h, :], rhs=vbt_[:, h * 64:(h + 1) * 64], start=True, stop=True))
                P.op("pe", [P_, kbg_], [pw], lambda e: e.matmul(pw[0:64, h * 128:(h + 1) * 128], lhsT=kbg_[:, h * 64:(h + 1) * 64], rhs=P_[:, h, :], start=True, stop=True))
            P.op("act", [pu], [usb_], lambda e: e.copy(out=usb_[:, :], in_=pu[:, 0:256]))
            P.op("dve", [pw], [wT_], lambda e: e.tensor_copy(out=wT_[:, :, :].rearrange("p h n -> p (h n)"), in_=pw[0:64, :]))
            if GST == 6:
                continue
            pws, po1 = next_ps(), next_ps()
            for h in range(4):
                hs = slice(h * 64, (h + 1) * 64)
                P.op("pe", [wT_, S4], [pws], lambda e: e.matmul(pws[:, hs], lhsT=wT_[:, h, :], rhs=S4[:, h, :], start=True, stop=True))
                P.op("pe", [q4_, S4], [po1], lambda e: e.matmul(po1[:, hs], lhsT=q4_[:, h, :], rhs=S4[:, h, :], start=True, stop=True))
            P.op("dve", [usb_, pws], [vnew_], lambda e: e.scalar_tensor_tensor(out=vnew_[:, :], in0=pws[:, 0:256], scalar=-1.0, in1=usb_[:, :], op0=ALU.mult, op1=ALU.add))
            po2, pd = next_ps(), next_ps()
            for h in range(4):
                hs = slice(h * 64, (h + 1) * 64)
                P.op("pe", [qk_, vnew_], [po2], lambda e: e.matmul(po2[:, hs], lhsT=qk_[:, h, :], rhs=vnew_[:, hs], start=True, stop=True))
                P.op("pe", [kdec_, vnew_], [pd], lambda e: e.matmul(pd[0:64, hs], lhsT=kdec_[:, hs], rhs=vnew_[:, hs], start=True, stop=True))
            P.op("act", [po2], [o2sb_], lambda e: e.copy(out=o2sb_[:, :], in_=po2[:, 0:256]))
            for h in range(4):
                hs = slice(h * 64, (h + 1) * 64)
                P.op("dve", [po1, sc_, o2sb_], [osb_], lambda e: e.scalar_tensor_tensor(out=osb_[:, hs], in0=po1[:, hs], scalar=sc_[:, 8 + h:9 + h], in1=o2sb_[:, hs], op0=ALU.mult, op1=ALU.add))
                P.op("dve", [S4, sc_, pd], [S4], lambda e: e.scalar_tensor_tensor(out=S4[:, h, :], in0=S4[:, h, :], scalar=sc_[0:64, 16 + h:17 + h], in1=pd[0:64, hs], op0=ALU.mult, op1=ALU.add))
            norm_gate_out(kb, P, next_ps, osb_, gsil_, osb2[i], sq[i], st4[i], y[i], ybf[i], 512, c)


def _rel_bucket_np(dist):
    n = np.maximum(dist, 0)
    nf = np.maximum(n, 1).astype(np.float32)
    large = 16 + (np.log(nf / 16) / math.log(128 / 16) * 16).astype(np.int32)
    large = np.minimum(large, 31)
    return np.where(n < 16, n, large)


def _host_consts():
    c = np.zeros((128, 6, 128), np.float32)
    c[:, 4, :] = np.tril(np.ones((128, 128), np.float32), -1)
    c[:, 5, :] = np.kron(np.eye(2, dtype=np.float32), np.ones((64, 64), np.float32))
    c[:, 0, :] = np.eye(128, dtype=np.float32)
    c[:, 1, :] = 1.0
    c[:, 2, :] = np.triu(np.ones((128, 128), np.float32))
    c[:, 3, :] = np.triu(np.ones((128, 128), np.float32), 1)
    return c


def prep_inputs(inputs, n_layers=DEPTH):
    f = lambda k: np.asarray(inputs[k], dtype=np.float32)
    perm = _col_perm()
    shared = {
        "w_in": np.ascontiguousarray(f("w_in")[:, :, perm]),
        "w_out": f("w_out"), "w_gate": f("w_gate"), "w_up": f("w_up"), "w_down": f("w_down"),
        "w_pg": f("w_ple_gate"), "w_pp": f("w_ple_proj"),
        "consts": _host_consts(),
    }
    import ml_dtypes
    shared["ind16"] = (np.arange(S)[None, :] // 256 == np.arange(16)[:, None]).astype(ml_dtypes.bfloat16)
    mcst = np.zeros((128, 3, 32, 16), np.float32)
    tt = np.arange(32)[:, None] // 2
    nn = np.arange(16)[None, :]
    mcst[:, 0] = np.where(nn < tt, 0.0, -1e30)
    mcst[:, 1] = (nn < tt)
    mcst[:, 2] = (nn == tt)
    shared["mconst"] = mcst.reshape(128, 3, 512)
    shared["gconv"] = np.ascontiguousarray(f("gdn_conv").reshape(DEPTH, 4, 6, 128).transpose(0, 3, 2, 1))
    shared["gdtb"] = f("gdn_dt_bias")
    shared["galog"] = f("gdn_a_log")
    shared["walpha"] = f("gla_w_alpha")
    shared["balpha"] = f("gla_b_alpha")
    shared["glanorm"] = f("gla_norm")
    shared["gdnnorm"] = f("gdn_norm")
    hmk = np.zeros((128, 8), np.float32)
    for h in range(4):
        hmk[32 * h:32 * (h + 1), h] = 32 ** -0.5
        hmk[64 * (h % 2):64 * (h % 2 + 1), 4 + h] = 1.0
    shared["hmask"] = hmk
    shared["dlam"] = np.ascontiguousarray(f("diff_lambda").reshape(DEPTH, 128))
    shared["dnorm"] = np.ascontiguousarray(f("diff_norm").reshape(DEPTH, 64, 1))
    g = np.zeros((DEPTH * 3 + 1, D), np.float32)
    for l in range(DEPTH):
        g[l * 3 + 0] = f("norm_mix")[l]
        g[l * 3 + 1] = f("norm_ffn")[l]
        g[l * 3 + 2] = f("norm_ple")[l]
    g[DEPTH * 3] = f("final_norm")
    shared["gains"] = np.ascontiguousarray(g.reshape(DEPTH * 3 + 1, KC, 128).transpose(2, 0, 1))
    kk = np.arange(128)[:, None]
    cc = np.arange(1024)[None, :]
    dist = cc - 384 - kk
    bidx = _rel_bucket_np(dist)
    rb = f("rel_bias")
    gt = np.empty((8, 128, 1024), np.float32)
    for h in range(8):
        gt[h] = np.where(dist < 0, np.float32(-BIG), rb[bidx, h])
    shared["gtab"] = gt
    x = f("x")
    p = f("p")
    maps = []
    for b in range(x.shape[0]):
        m = dict(shared)
        m["xT"] = np.ascontiguousarray(x[b].T)
        m["pT"] = np.ascontiguousarray(p[:, b].transpose(0, 2, 1))
        maps.append(m)
    return maps


_NC_CACHE = {}


def kernel(**inputs):
    maps = prep_inputs(inputs)
    if "nc" not in _NC_CACHE:
        _NC_CACHE["nc"] = build_program()
    nc = _NC_CACHE["nc"]
    res = run_bass_kernel_spmd(nc, maps, core_ids=list(range(8)))
    out = np.stack([np.ascontiguousarray(r["outT"].T) for r in res.results], axis=0)
    return out.astype(np.float32)
```

```python
import contextlib
import math
import numpy as np
import concourse.bass as bass
import concourse.mybir as mybir
from concourse.bass_utils import run_bass_kernel_spmd

F32 = mybir.dt.float32
BF16 = mybir.dt.bfloat16
AF = mybir.ActivationFunctionType
ALU = mybir.AluOpType
AX = mybir.AxisListType

EPOCH = 30000
DMA_EPOCH = 2000
DMA_RING = 8
NSEM_POOL = 96


class Buf:
    __slots__ = ("name", "w", "r")

    def __init__(self, name=""):
        self.name = name
        self.w = None
        self.r = []


class Tile(Buf):
    __slots__ = ("t",)

    def __init__(self, t, name=""):
        Buf.__init__(self, name)
        self.t = t

    def __getitem__(self, k):
        return self.t[k]


class Prog:
    def __init__(self, nc):
        self.nc = nc
        self.stack = contextlib.ExitStack()
        self.eobj = {"pe": nc.tensor, "act": nc.scalar, "dve": nc.vector,
                     "pool": nc.gpsimd, "sp": nc.sync}
        self.esem = {}
        self.ecnt = {}
        self.eepoch = {}
        self.waited = {e: {} for e in self.eobj}
        self.sems = {}
        self.semval = {}
        self.nsem = 0
        for e in self.eobj:
            self.eepoch[e] = -1
            self._new_epoch(e)
        self.dring = []
        self.dpos = 0
        for i in range(DMA_RING):
            self.dring.append(self._new_dsem())
        self.uid = 0
        self.ninst = 0

    def _alloc_sem(self, key):
        if not hasattr(self, "sempool"):
            self.sempool = [self.stack.enter_context(self.nc.semaphore("s%d" % i)) for i in range(NSEM_POOL)]
        h = self.sempool[self.nsem]
        self.nsem += 1
        self.sems[key] = h
        self.semval[key] = 0
        return h

    def _new_epoch(self, e):
        self.eepoch[e] += 1
        key = (e, self.eepoch[e])
        self._alloc_sem(key)
        self.esem[e] = key
        self.ecnt[e] = 0

    def _new_dsem(self):
        key = ("dma", self.nsem)
        self._alloc_sem(key)
        return [key, 0]

    def _wait(self, eng, key, val):
        w = self.waited[eng]
        if w.get(key, 0) >= val:
            return
        w[key] = val
        self.eobj[eng].wait_ge(self.sems[key], val)

    def _deps(self, eng, reads, writes, is_pe=False):
        for b in reads:
            if b.w is not None:
                key, val, we = b.w
                self._wait(eng, key, val)
        for b in writes:
            if b.w is not None:
                key, val, we = b.w
                if not (is_pe and we == "pe"):
                    self._wait(eng, key, val)
            for (key, val, re_) in b.r:
                if re_ == eng:
                    continue
                self._wait(eng, key, val)

    def _mark(self, ev, reads, writes):
        for b in reads:
            b.r.append(ev)
        for b in writes:
            b.w = ev
            b.r = []

    def op(self, eng, reads, writes, fn):
        self._deps(eng, reads, writes, is_pe=(eng == "pe"))
        if self.ecnt[eng] >= EPOCH:
            self._new_epoch(eng)
        ins = fn(self.eobj[eng])
        key = self.esem[eng]
        self.ecnt[eng] += 1
        val = self.ecnt[eng]
        ins.then_inc(self.sems[key], 1)
        self.semval[key] = val
        self._mark((key, val, eng), reads, writes)
        self.ninst += 1
        return ins

    def dma(self, q, out, in_, reads, writes, **kw):
        for b in reads:
            if b.w is not None:
                self._wait(q, b.w[0], b.w[1])
        for b in writes:
            if b.w is not None:
                self._wait(q, b.w[0], b.w[1])
            for (key, val, re_) in b.r:
                self._wait(q, key, val)
        slot = self.dring[self.dpos]
        if slot[1] >= DMA_EPOCH:
            slot = self._new_dsem()
            self.dring[self.dpos] = slot
        key = slot[0]
        if slot[1] > 0:
            self._wait(q, key, 16 * slot[1])
        self.dpos = (self.dpos + 1) % DMA_RING
        ins = self.eobj[q].dma_start(out=out, in_=in_, **kw)
        slot[1] += 1
        val = 16 * slot[1]
        ins.then_inc(self.sems[key], 16)
        self.semval[key] = val
        self._mark((key, val, "dma"), reads, writes)
        self.ninst += 1
        return ins

    def barrier(self):
        for e in self.eobj:
            for key, val in list(self.semval.items()):
                if val > 0:
                    self._wait(e, key, val)

    def finish(self):
        self.barrier()

    def sbuf(self, st, shape, dtype, name=None):
        self.uid += 1
        nm = "%s_%d" % (name or "t", self.uid)
        t = st.enter_context(self.nc.sbuf_tensor(nm, list(shape), dtype))
        return Tile(t, nm)

    def psum(self, st, shape, dtype, name=None):
        self.uid += 1
        nm = "%s_%d" % (name or "p", self.uid)
        t = st.enter_context(self.nc.psum_tensor(nm, list(shape), dtype))
        return Tile(t, nm)

    def dram(self, shape, dtype, name, kind="Internal"):
        t = self.nc.dram_tensor(name, list(shape), dtype, kind=kind)
        return Tile(t.ap() if hasattr(t, "ap") else t, name)


D = 1024
S = 4096
DEPTH = 4
TB = 1024
NTB = S // TB
KC = D // 128
DFF = 2816
FC = DFF // 128
PLE = 256
EPS = 1e-6
BIG = 30000.0
NFM = 2064
NTM = 1416
NIN = NFM + NTM
FM_AQ, FM_AK, FM_BQ, FM_BK, FM_C, FM_DQ, FM_DK, FM_DLR = 0, 256, 512, 768, 1024, 1792, 1920, 2048
TM_AV, TM_BV, TM_CZ, TM_DV, TM_DG, TM_DK, TM_CA, TM_CB = 0, 256, 512, 768, 1024, 1280, 1408, 1412

_SW = (256, 256, 256, 256, 256, 256, 256, 256, 256, 256, 4, 4, 128, 128, 256, 256, 16)
_OFF = np.concatenate([[0], np.cumsum(_SW)])
(O_AQ, O_AK, O_AV, O_BQ, O_BK, O_BV, O_CQ, O_CK, O_CV, O_CZ, O_CA, O_CB, O_DQ, O_DK, O_DV,
 O_DG, O_DLR) = [int(v) for v in _OFF[:-1]]


def _col_perm():
    r = lambda o, n: list(range(o, o + n))
    fm = (r(O_AQ, 256) + r(O_AK, 256) + r(O_BQ, 256) + r(O_BK, 256) + r(O_CQ, 768)
          + r(O_DQ, 128) + r(O_DK, 128) + r(O_DLR, 16))
    tm = (r(O_AV, 256) + r(O_BV, 256) + r(O_CZ, 256) + r(O_DV, 256) + r(O_DG, 256)
          + r(O_DK, 128) + r(O_CA, 4) + r(O_CB, 4))
    assert len(fm) == NFM and len(tm) == NTM
    return np.array(fm + tm)


class K:
    pass


def build_program(n_layers=DEPTH, mixers=("a", "b", "c", "d"), debug_out=False, mixer_only=False):
    nc = bass.Bass("TRN2", target_bir_lowering=False)
    P = Prog(nc)

    def din(name, shape, dt=F32):
        return Tile(nc.dram_tensor(name, list(shape), dt, kind="ExternalInput").ap(), name)

    xT = din("xT", [D, S])
    pT = din("pT", [DEPTH, PLE, S])
    w_in = din("w_in", [DEPTH, D, NIN])
    w_out = din("w_out", [DEPTH, D, D])
    w_gate = din("w_gate", [DEPTH, D, DFF])
    w_up = din("w_up", [DEPTH, D, DFF])
    w_down = din("w_down", [DEPTH, DFF, D])
    w_pg = din("w_pg", [DEPTH, D, D])
    w_pp = din("w_pp", [DEPTH, PLE, D])
    gains = din("gains", [128, DEPTH * 3 + 1, KC])
    gtab = din("gtab", [8, 128, 1024])
    consts = din("consts", [128, 6, 128])
    ind16 = din("ind16", [16, S], BF16)
    mconst = din("mconst", [128, 3, 512])
    dlam = din("dlam", [DEPTH, 128])
    walpha = din("walpha", [DEPTH, 16, 128])
    balpha = din("balpha", [DEPTH, 128])
    glanorm = din("glanorm", [DEPTH, 64])
    gdnnorm = din("gdnnorm", [DEPTH, 64])
    hmask = din("hmask", [128, 8])
    gconv = din("gconv", [DEPTH, 128, 6, 4])
    gdtb = din("gdtb", [DEPTH, 4])
    galog = din("galog", [DEPTH, 4])
    dnorm = din("dnorm", [DEPTH, 64, 1])
    outT = Tile(nc.dram_tensor("outT", [D, S], F32, kind="ExternalOutput").ap(), "outT")

    def dscr(name, shape, dt):
        return Tile(nc.dram_tensor(name, list(shape), dt, kind="Internal").ap(), name)

    wb_in = dscr("wb_in", [DEPTH, D, NIN], BF16)
    wb_out = dscr("wb_out", [DEPTH, D, D], BF16)
    wb_gate = dscr("wb_gate", [DEPTH, D, DFF], BF16)
    wb_up = dscr("wb_up", [DEPTH, D, DFF], BF16)
    wb_down = dscr("wb_down", [DEPTH, DFF, D], BF16)
    wb_pg = dscr("wb_pg", [DEPTH, D, D], BF16)
    wb_pp = dscr("wb_pp", [DEPTH, PLE, D], BF16)
    hT = dscr("hT", [D, S], F32)
    FMS = dscr("FMS", [NFM, S], F32)
    TMS = dscr("TMS", [S, NTM], F32)
    if debug_out:
        yT = Tile(nc.dram_tensor("yT", [D, S], BF16, kind="ExternalOutput").ap(), "yT")
    else:
        yT = dscr("yT", [D, S], BF16)

    with P.stack, contextlib.ExitStack() as gst:
        cst = P.sbuf(gst, [128, 6, 128], F32, "cst")
        P.dma("sp", cst[:, :, :], consts[:, :, :], [consts], [cst])
        ident = cst.t[:, 0, :]
        ones = cst.t[:, 1, :]
        gn = P.sbuf(gst, [128, DEPTH * 3 + 1, KC], F32, "gn")
        P.dma("sp", gn[:, :, :], gains[:, :, :], [gains], [gn])
        identb = P.sbuf(gst, [128, 128], BF16, "identb")
        P.op("dve", [cst], [identb], lambda e: e.tensor_copy(out=identb[:, :], in_=ident))
        psums = [P.psum(gst, [128, 512], F32, "ps%d" % i) for i in range(7)]
        psacc = P.psum(gst, [128, 512], F32, "psacc")
        pstate = [0]

        def next_ps():
            t = psums[pstate[0] % 7]
            pstate[0] += 1
            return t

        with contextlib.ExitStack() as st:
            stg = [P.sbuf(st, [128, 2048], F32, "wstg") for _ in range(3)]
            cbf = [P.sbuf(st, [128, 2048], BF16, "wcb") for _ in range(3)]
            cnt = [0]

            def cast2d(src, dst, l, Kdim, N):
                for kc in range(Kdim // 128):
                    for c0 in range(0, N, 2048):
                        w = min(2048, N - c0)
                        i = cnt[0] % 3
                        cnt[0] += 1
                        s_, c_ = stg[i], cbf[i]
                        P.dma("sp", s_[:, 0:w], src.t[l, kc * 128:(kc + 1) * 128, c0:c0 + w], [], [s_])
                        eng = ("dve", "pool", "act")[cnt[0] % 3]
                        if eng == "act":
                            P.op("act", [s_], [c_], lambda e: e.copy(out=c_[:, 0:w], in_=s_[:, 0:w]))
                        else:
                            P.op(eng, [s_], [c_], lambda e: e.tensor_copy(out=c_[:, 0:w], in_=s_[:, 0:w]))
                        P.dma("pool", dst.t[l, kc * 128:(kc + 1) * 128, c0:c0 + w], c_[:, 0:w], [c_], [])

            for l in range(n_layers):
                cast2d(w_in, wb_in, l, D, NIN)
                if mixer_only:
                    continue
                cast2d(w_out, wb_out, l, D, D)
                cast2d(w_gate, wb_gate, l, D, DFF)
                cast2d(w_up, wb_up, l, D, DFF)
                cast2d(w_down, wb_down, l, DFF, D)
                cast2d(w_pg, wb_pg, l, D, D)
                cast2d(w_pp, wb_pp, l, PLE, D)
            P.barrier()

        kb = K()
        kb.P, kb.nc, kb.next_ps = P, nc, next_ps
        kb.ident, kb.ones, kb.identb, kb.cst, kb.gn = ident, ones, identb, cst, gn
        kb.FMS, kb.TMS, kb.yT, kb.hT, kb.gtab = FMS, TMS, yT, hT, gtab
        kb.psacc, kb.ind16, kb.mconst, kb.dlam, kb.dnorm = psacc, ind16, mconst, dlam, dnorm
        kb.walpha, kb.balpha, kb.glanorm, kb.gdnnorm, kb.hmask = walpha, balpha, glanorm, gdnnorm, hmask
        kb.gconv, kb.gdtb, kb.galog = gconv, gdtb, galog

        for l in range(n_layers + 1):
            dense_phase(kb, l, n_layers, xT, pT, outT,
                        wb_in, wb_out, wb_gate, wb_up, wb_down, wb_pg, wb_pp, have_y=bool(mixers))
            P.barrier()
            if l < n_layers:
                mixer_phase(kb, l, mixers)
                P.barrier()
            if mixer_only:
                break
        P.finish()
    return nc


def wstream(P, bufs, loads, computes):
    n = len(loads)
    nb = len(bufs)
    for i in range(min(nb, n)):
        loads[i](bufs[i % nb])
    for i in range(n):
        computes[i](bufs[i % nb])
        if i + nb < n:
            loads[i + nb](bufs[i % nb])


def dense_phase(kb, l, n_layers, xT, pT, outT, wb_in, wb_out, wb_gate, wb_up, wb_down, wb_pg, wb_pp,
                have_y=True):
    P, nc, next_ps = kb.P, kb.nc, kb.next_ps
    prev = l - 1
    with contextlib.ExitStack() as st:
        hblk = P.sbuf(st, [128, KC, TB], F32, "hblk")
        xn = P.sbuf(st, [128, KC, TB], BF16, "xn")
        wt = [P.sbuf(st, [128, KC, 512], BF16, "wt") for _ in range(2)]
        tmpf = [P.sbuf(st, [128, 512], F32, "tmpf") for _ in range(4)]
        tcnt = [0]

        def next_tmp():
            t = tmpf[tcnt[0] % 4]
            tcnt[0] += 1
            return t

        if prev >= 0:
            act = P.sbuf(st, [128, FC, TB], BF16, "act")
            wd = [P.sbuf(st, [128, FC, 256], BF16, "wd") for _ in range(2)]
            pst = P.sbuf(st, [128, TB], F32, "pst")
            pb = P.sbuf(st, [128, 2, TB], BF16, "pb")
            gt = [P.sbuf(st, [128, 512], F32, "gt") for _ in range(2)]

        def norm_block(which):
            for sb in range(TB // 512):
                sl = slice(sb * 512, (sb + 1) * 512)
                ps = next_ps()
                for kc in range(KC):
                    sq = next_tmp()
                    P.op("act", [hblk], [sq], lambda e: e.activation(out=sq[:, :], in_=hblk[:, kc, sl], func=AF.Square))
                    P.op("pe", [sq, kb.cst], [ps], lambda e: e.matmul(ps[:, :], lhsT=kb.ones, rhs=sq[:, :],
                                                                     start=(kc == 0), stop=(kc == KC - 1)))
                rstd = next_tmp()
                P.op("dve", [ps], [rstd], lambda e: e.tensor_scalar(out=rstd[:, :], in0=ps[:, :], scalar1=1.0 / D,
                                                                   scalar2=EPS, op0=ALU.mult, op1=ALU.add))
                P.op("act", [rstd], [rstd], lambda e: e.activation(out=rstd[:, :], in_=rstd[:, :], func=AF.Sqrt))
                P.op("dve", [rstd], [rstd], lambda e: e.reciprocal(out=rstd[:, :], in_=rstd[:, :]))
                for kc in range(KC):
                    P.op("dve", [hblk, rstd, kb.gn], [xn], lambda e: e.scalar_tensor_tensor(
                        out=xn[:, kc, sl], in0=hblk[:, kc, sl], scalar=kb.gn[:, which, kc:kc + 1], in1=rstd[:, :],
                        op0=ALU.mult, op1=ALU.mult))
            return rstd

        def load_w(buf, Wb, lidx, kcn, c0, w):
            P.dma("sp", buf[:, 0:kcn, 0:w],
                  Wb.t[lidx, 0:kcn * 128, c0:c0 + w].rearrange("(kc p) n -> p kc n", p=128), [], [buf])

        def fm_linear(src, kcn, Wb, lidx, N, bufs, evac, wmax=512):
            loads, comps = [], []
            for c0 in range(0, N, wmax):
                w = min(wmax, N - c0)
                loads.append(lambda buf, c0=c0, w=w: load_w(buf, Wb, lidx, kcn, c0, w))

                def comp(buf, c0=c0, w=w):
                    for m in range(0, w, 128):
                        msz = min(128, w - m)
                        for sb in range(TB // 512):
                            ps = next_ps()
                            for kc in range(kcn):
                                P.op("pe", [buf, src], [ps], lambda e: e.matmul(
                                    ps[0:msz, :], lhsT=buf[:, kc, m:m + msz], rhs=src[:, kc, sb * 512:(sb + 1) * 512],
                                    start=(kc == 0), stop=(kc == kcn - 1)))
                            evac(ps, c0 + m, msz, sb)
                comps.append(comp)
            wstream(P, bufs, loads, comps)

        for tb in range(NTB):
            t0 = tb * TB
            hsrc = xT if l <= 1 else kb.hT
            P.dma("sp", hblk[:, :, :], hsrc.t[:, t0:t0 + TB].rearrange("(kc p) t -> p kc t", p=128), [], [hblk])
            if prev >= 0:
                def ev_res(ps, n0, nsz, sb):
                    kc = n0 // 128
                    sl = slice(sb * 512, (sb + 1) * 512)
                    P.op("dve", [ps, hblk], [hblk], lambda e: e.tensor_tensor(
                        out=hblk[:, kc, sl], in0=ps[:, :], in1=hblk[:, kc, sl], op=ALU.add))
                if have_y:
                    P.dma("pool", xn[:, :, :], kb.yT.t[:, t0:t0 + TB].rearrange("(kc p) t -> p kc t", p=128), [], [xn])
                    fm_linear(xn, KC, wb_out, prev, D, wt, ev_res)
                norm_block(prev * 3 + 1)
                gate_keep = {}

                def ev_gate(ps, n0, nsz, sb):
                    g_ = gt[sb % 2]
                    P.op("act", [ps], [g_], lambda e: e.activation(out=g_[:, :], in_=ps[:, :], func=AF.Silu))
                    gate_keep[(n0, sb)] = g_

                def ev_up(ps, n0, nsz, sb):
                    g_ = gate_keep[(n0, sb)]
                    fc = n0 // 128
                    P.op("dve", [ps, g_], [act], lambda e: e.tensor_tensor(
                        out=act[:, fc, sb * 512:(sb + 1) * 512], in0=ps[:, :], in1=g_[:, :], op=ALU.mult))

                loads, comps = [], []
                for fc in range(FC):
                    for which, Wb, ev in ((0, wb_gate, ev_gate), (1, wb_up, ev_up)):
                        loads.append(lambda buf, fc=fc, Wb=Wb: load_w(buf, Wb, prev, KC, fc * 128, 128))

                        def comp(buf, fc=fc, ev=ev):
                            for sb in range(TB // 512):
                                ps = next_ps()
                                for kc in range(KC):
                                    P.op("pe", [buf, xn], [ps], lambda e: e.matmul(
                                        ps[:, :], lhsT=buf[:, kc, 0:128], rhs=xn[:, kc, sb * 512:(sb + 1) * 512],
                                        start=(kc == 0), stop=(kc == KC - 1)))
                                ev(ps, fc * 128, 128, sb)
                        comps.append(comp)
                wstream(P, wt, loads, comps)
                fm_linear(act, FC, wb_down, prev, D, wd, ev_res, wmax=256)
                norm_block(prev * 3 + 2)
                for c in range(2):
                    P.dma("pool", pst[:, :], pT.t[prev, c * 128:(c + 1) * 128, t0:t0 + TB], [], [pst])
                    P.op("pool", [pst], [pb], lambda e: e.tensor_copy(out=pb[:, c, :], in_=pst[:, :]))
                sig_keep = {}

                def ev_sig(ps, n0, nsz, sb):
                    g_ = gt[sb % 2]
                    P.op("act", [ps], [g_], lambda e: e.activation(out=g_[:, :], in_=ps[:, :], func=AF.Sigmoid))
                    sig_keep[(n0, sb)] = g_

                def ev_ple(ps, n0, nsz, sb):
                    g_ = sig_keep[(n0, sb)]
                    kc = n0 // 128
                    sl = slice(sb * 512, (sb + 1) * 512)
                    t_ = next_tmp()
                    P.op("dve", [ps, g_], [t_], lambda e: e.tensor_tensor(out=t_[:, :], in0=ps[:, :], in1=g_[:, :], op=ALU.mult))
                    P.op("pool", [t_, hblk], [hblk], lambda e: e.tensor_tensor(
                        out=hblk[:, kc, sl], in0=t_[:, :], in1=hblk[:, kc, sl], op=ALU.add))

                loads, comps = [], []
                for oc in range(KC):
                    loads.append(lambda buf, oc=oc: load_w(buf, wb_pg, prev, KC, oc * 128, 128))

                    def comp_g(buf, oc=oc):
                        for sb in range(TB // 512):
                            ps = next_ps()
                            for kc in range(KC):
                                P.op("pe", [buf, xn], [ps], lambda e: e.matmul(
                                    ps[:, :], lhsT=buf[:, kc, 0:128], rhs=xn[:, kc, sb * 512:(sb + 1) * 512],
                                    start=(kc == 0), stop=(kc == KC - 1)))
                            ev_sig(ps, oc * 128, 128, sb)
                    comps.append(comp_g)
                    loads.append(lambda buf, oc=oc: load_w(buf, wb_pp, prev, 2, oc * 128, 128))

                    def comp_p(buf, oc=oc):
                        for sb in range(TB // 512):
                            ps = next_ps()
                            for kc in range(2):
                                P.op("pe", [buf, pb], [ps], lambda e: e.matmul(
                                    ps[:, :], lhsT=buf[:, kc, 0:128], rhs=pb[:, kc, sb * 512:(sb + 1) * 512],
                                    start=(kc == 0), stop=(kc == 1)))
                            ev_ple(ps, oc * 128, 128, sb)
                    comps.append(comp_p)
                wstream(P, wt, loads, comps)

            if l == n_layers:
                for sb in range(TB // 512):
                    sl = slice(sb * 512, (sb + 1) * 512)
                    ps = next_ps()
                    for kc in range(KC):
                        sq = next_tmp()
                        P.op("act", [hblk], [sq], lambda e: e.activation(out=sq[:, :], in_=hblk[:, kc, sl], func=AF.Square))
                        P.op("pe", [sq, kb.cst], [ps], lambda e: e.matmul(ps[:, :], lhsT=kb.ones, rhs=sq[:, :],
                                                                         start=(kc == 0), stop=(kc == KC - 1)))
                    rstd = next_tmp()
                    P.op("dve", [ps], [rstd], lambda e: e.tensor_scalar(out=rstd[:, :], in0=ps[:, :], scalar1=1.0 / D,
                                                                       scalar2=EPS, op0=ALU.mult, op1=ALU.add))
                    P.op("act", [rstd], [rstd], lambda e: e.activation(out=rstd[:, :], in_=rstd[:, :], func=AF.Sqrt))
                    P.op("dve", [rstd], [rstd], lambda e: e.reciprocal(out=rstd[:, :], in_=rstd[:, :]))
                    for kc in range(KC):
                        P.op("dve", [hblk, rstd, kb.gn], [hblk], lambda e: e.scalar_tensor_tensor(
                            out=hblk[:, kc, sl], in0=hblk[:, kc, sl], scalar=kb.gn[:, DEPTH * 3, kc:kc + 1], in1=rstd[:, :],
                            op0=ALU.mult, op1=ALU.mult))
                P.dma("sp", outT.t[:, t0:t0 + TB].rearrange("(kc p) t -> p kc t", p=128), hblk[:, :, :], [hblk], [outT])
                continue

            if prev >= 0:
                P.dma("pool", kb.hT.t[:, t0:t0 + TB].rearrange("(kc p) t -> p kc t", p=128), hblk[:, :, :], [hblk], [])
            norm_block(l * 3 + 0)

            def ev_fm(ps, n0, nsz, sb):
                t_ = next_tmp()
                P.op("act", [ps], [t_], lambda e: e.copy(out=t_[0:nsz, :], in_=ps[0:nsz, :]))
                P.dma("pool", kb.FMS.t[n0:n0 + nsz, t0 + sb * 512:t0 + (sb + 1) * 512], t_[0:nsz, :], [t_], [])
            fm_linear(xn, KC, wb_in, l, NFM, wt, ev_fm)
            loads, comps = [], []
            for c0 in range(0, NTM, 512):
                w = min(512, NTM - c0)
                loads.append(lambda buf, c0=c0, w=w: load_w(buf, wb_in, l, KC, NFM + c0, w))

                def comp_tm(buf, c0=c0, w=w):
                    for tt in range(TB // 128):
                        ps = next_ps()
                        for kc in range(KC):
                            P.op("pe", [buf, xn], [ps], lambda e: e.matmul(
                                ps[:, 0:w], lhsT=xn[:, kc, tt * 128:(tt + 1) * 128], rhs=buf[:, kc, 0:w],
                                start=(kc == 0), stop=(kc == KC - 1)))
                        t_ = next_tmp()
                        P.op("dve", [ps], [t_], lambda e: e.tensor_copy(out=t_[:, 0:w], in_=ps[:, 0:w]))
                        P.dma("pool", kb.TMS.t[t0 + tt * 128:t0 + (tt + 1) * 128, c0:c0 + w], t_[:, 0:w], [t_], [])
                comps.append(comp_tm)
            wstream(P, wt, loads, comps)


def mixer_phase(kb, l, mixers):
    if "a" in mixers or "b" in mixers:
        attn_phase(kb, l, mixers)
        kb.P.barrier()
    if "d" in mixers:
        gla_phase(kb, l)
        kb.P.barrier()
    if "c" in mixers:
        gdn_phase(kb, l)
        kb.P.barrier()


def attn_phase(kb, l, mixers):
    P, nc, next_ps = kb.P, kb.nc, kb.next_ps
    lam_init = 0.8 - 0.6 * math.exp(-0.3 * l)
    with contextlib.ExitStack() as st:
        G = P.sbuf(st, [128, 1024], F32, "G")
        stg = P.sbuf(st, [64, S], F32, "stg")
        kstg = P.sbuf(st, [64, S], F32, "kstg")
        QT = P.sbuf(st, [80, S], BF16, "QT")
        KT = P.sbuf(st, [80, S], BF16, "KT")
        MT = P.sbuf(st, [16, S], BF16, "MT")
        IND = P.sbuf(st, [16, S], BF16, "IND")
        vstg = P.sbuf(st, [128, 32, 64], F32, "vstg")
        Vaug = P.sbuf(st, [128, 32, 128], BF16, "Vaug")
        PT = [P.sbuf(st, [128, 512], BF16, "PT") for _ in range(3)]
        stmp = [P.sbuf(st, [128, 512], F32, "stmp") for _ in range(2)]
        rec = P.sbuf(st, [128, 512], F32, "rec")
        on = [P.sbuf(st, [64, 512], F32, "on") for _ in range(2)]
        osb = P.sbuf(st, [64, 512], F32, "osb")
        ob = [P.sbuf(st, [64, 512], BF16, "ob") for _ in range(2)]
        sm = P.sbuf(st, [64, 16], F32, "sm")
        lamb = P.sbuf(st, [64, 128], F32, "lamb")
        lamp = P.sbuf(st, [64, 128], F32, "lamp")
        mc = P.sbuf(st, [128, 3, 512], F32, "mc")
        gm = P.sbuf(st, [128, 32, 16], F32, "gm")
        m8 = P.sbuf(st, [128, 32, 8], F32, "m8")
        sel = P.sbuf(st, [128, 32, 16], F32, "sel")
        km = P.sbuf(st, [64, 16], F32, "km")
        acc = kb.psacc
        cnt = [0]
        oncnt = [0]
        P.op("pool", [], [Vaug], lambda e: e.memset(Vaug[:, :, :], 1.0))
        P.dma("sp", IND[:, :], kb.ind16.t[:, :], [], [IND])
        P.dma("sp", mc[:, :, :], kb.mconst.t[:, :, :], [], [mc])
        if "b" in mixers:
            P.dma("sp", lamb[:, :], kb.dlam.t[l:l + 1, :].broadcast_to([64, 128]), [], [lamb])
            P.dma("sp", sm[:, 4:5], kb.dnorm.t[l, :, :], [], [sm])
            P.op("dve", [lamb], [lamp], lambda e: e.tensor_tensor(out=lamp[:, 0:32], in0=lamb[:, 0:32], in1=lamb[:, 32:64], op=ALU.mult))
            P.op("dve", [lamb, lamp], [lamp], lambda e: e.tensor_tensor(out=lamp[:, 32:64], in0=lamb[:, 64:96], in1=lamb[:, 96:128], op=ALU.mult))
            P.op("dve", [lamp], [sm], lambda e: e.tensor_reduce(out=sm[:, 0:2], in_=lamp[:, 0:64].rearrange("p (a b) -> p a b", b=32), axis=AX.X, op=ALU.add))
            P.op("act", [sm], [sm], lambda e: e.activation(out=sm[:, 0:2], in_=sm[:, 0:2], func=AF.Exp))
            P.op("dve", [sm], [sm], lambda e: e.tensor_tensor(out=sm[:, 2:3], in0=sm[:, 1:2], in1=sm[:, 0:1], op=ALU.subtract))
            P.op("dve", [sm], [sm], lambda e: e.tensor_scalar(out=sm[:, 2:3], in0=sm[:, 2:3], scalar1=-lam_init, scalar2=None, op0=ALU.add))
            P.op("dve", [sm], [sm], lambda e: e.tensor_scalar(out=sm[:, 5:6], in0=sm[:, 4:5], scalar1=(1.0 - lam_init), scalar2=None, op0=ALU.mult))

        def load_head(qoff, koff, voff, h):
            P.dma("sp", G[:, :], kb.gtab.t[h, :, :], [], [G])
            P.dma("sp", stg[:, :], kb.FMS.t[qoff:qoff + 64, :], [], [stg])
            P.dma("pool", kstg[:, :], kb.FMS.t[koff:koff + 64, :], [], [kstg])
            for q4i in range(4):
                P.dma("sp", vstg[:, q4i * 8:(q4i + 1) * 8, :], kb.TMS.t[q4i * 1024:(q4i + 1) * 1024, voff:voff + 64].rearrange("(t p) c -> p t c", p=128), [], [vstg])
            for c in range(4):
                sl = slice(c * 1024, (c + 1) * 1024)
                P.op("pool", [stg], [QT], lambda e: e.tensor_copy(out=QT[0:64, sl], in_=stg[:, sl]))
                P.op("dve", [kstg], [KT], lambda e: e.tensor_copy(out=KT[0:64, sl], in_=kstg[:, sl]))
            P.op("pool", [vstg], [Vaug], lambda e: e.tensor_copy(out=Vaug[:, :, 0:64], in_=vstg[:, :, :]))

        def moba_select():
            import os as _os
            msel = int(_os.environ.get("MOBA_SEL", 9))
            if msel == 0:
                P.op("dve", [], [MT], lambda e: e.memset(MT[:, :], 0.0))
                return
            P.op("dve", [kstg], [km], lambda e: e.tensor_reduce(out=km[:, :], in_=kstg[:, :].rearrange("p (a b) -> p a b", b=256), axis=AX.X, op=ALU.add))
            P.op("dve", [km], [km], lambda e: e.tensor_scalar(out=km[:, :], in0=km[:, :], scalar1=1.0 / 256, scalar2=None, op0=ALU.mult))
            ps = next_ps()
            for t in range(32):
                P.op("pe", [stg, km], [ps], lambda e: e.matmul(ps[:, t * 16:(t + 1) * 16], lhsT=stg[:, t * 128:(t + 1) * 128], rhs=km[:, :], start=True, stop=True))
            gmf = gm[:, :, :].rearrange("p a b -> p (a b)")
            P.op("dve", [ps, mc], [gm], lambda e: e.tensor_tensor(out=gmf, in0=ps[:, :], in1=mc[:, 0, :], op=ALU.add))
            if msel == 1:
                P.op("dve", [], [MT], lambda e: e.memset(MT[:, :], 0.0))
                return
            for t in range(32):
                P.op("dve", [gm], [m8], lambda e: e.max(out=m8[:, t, :], in_=gm[:, t, :]))
            if msel == 2:
                P.op("dve", [], [MT], lambda e: e.memset(MT[:, :], 0.0))
                return
            P.op("dve", [gm, m8], [sel], lambda e: e.tensor_tensor(out=sel[:, :, :], in0=gm[:, :, :], in1=m8[:, :, 2:3].broadcast_to([128, 32, 16]), op=ALU.is_ge))
            self_f = sel[:, :, :].rearrange("p a b -> p (a b)")
            P.op("dve", [sel, mc], [sel], lambda e: e.tensor_tensor(out=self_f, in0=self_f, in1=mc[:, 1, :], op=ALU.mult))
            P.op("dve", [sel, mc], [sel], lambda e: e.tensor_tensor(out=self_f, in0=self_f, in1=mc[:, 2, :], op=ALU.max))
            P.op("dve", [sel], [sel], lambda e: e.tensor_scalar(out=self_f, in0=self_f, scalar1=-1.0, scalar2=BIG, op0=ALU.add, op1=ALU.mult))
            if msel == 3:
                P.op("dve", [], [MT], lambda e: e.memset(MT[:, :], 0.0))
                return
            for g4 in range(8):
                ps2 = next_ps()
                for j in range(4):
                    t = g4 * 4 + j
                    P.op("pe", [sel, kb.cst], [ps2], lambda e: e.matmul(ps2[0:16, j * 128:(j + 1) * 128], lhsT=sel[:, t, :], rhs=kb.ident, start=True, stop=True))
                P.op("dve", [ps2], [MT], lambda e: e.tensor_copy(out=MT[:, g4 * 512:(g4 + 1) * 512], in_=ps2[0:16, :]))

        def attend(rows, qb, scale, masked=False):
            a = acc
            nkt = 4 * qb + 4
            LOOK = 2
            spsd = {}

            def issue_qk(kt):
                sps = next_ps()
                spsd[kt] = sps
                P.op("pe", [KT, QT], [sps], lambda e: e.matmul(sps[:, :], lhsT=KT[rows, kt * 128:(kt + 1) * 128], rhs=QT[rows, qb * 512:(qb + 1) * 512], start=True, stop=not masked))
                if masked:
                    P.op("pe", [IND, MT], [sps], lambda e: e.matmul(sps[:, :], lhsT=IND[:, kt * 128:(kt + 1) * 128], rhs=MT[:, qb * 512:(qb + 1) * 512], start=False, stop=True))

            for kt in range(min(LOOK, nkt)):
                issue_qk(kt)
            for kt in range(nkt):
                sps = spsd.pop(kt)
                pt = PT[cnt[0] % 3]
                cnt[0] += 1
                r = kt - 4 * qb
                if r < -1:
                    P.op("act", [sps, G], [pt], lambda e: e.activation(out=pt[:, :], in_=sps[:, :], func=AF.Exp, bias=G[:, 1023:1024], scale=scale))
                else:
                    c0 = 384 - 128 * r
                    t_ = stmp[cnt[0] % 2]
                    P.op("dve", [sps, G], [t_], lambda e: e.scalar_tensor_tensor(out=t_[:, :], in0=sps[:, :], scalar=scale, in1=G[:, c0:c0 + 512], op0=ALU.mult, op1=ALU.add))
                    P.op("act", [t_], [pt], lambda e: e.activation(out=pt[:, :], in_=t_[:, :], func=AF.Exp))
                if kt + LOOK < nkt:
                    issue_qk(kt + LOOK)
                P.op("pe", [Vaug, pt], [a], lambda e: e.matmul(a[:, :], lhsT=Vaug[:, kt, :], rhs=pt[:, :], start=(kt == 0), stop=(kt == nkt - 1)))
            oncnt[0] += 1
            o_ = on[oncnt[0] % 2]
            P.op("dve", [a], [rec], lambda e: e.reciprocal(out=rec[64:128, :], in_=a[64:128, :]))
            P.op("dve", [a, rec], [o_], lambda e: e.tensor_tensor(out=o_[:, :], in0=a[0:64, :], in1=rec[64:128, :], op=ALU.mult))
            return o_

        ocnt = [0]
        if "a" in mixers:
            for h in range(4):
                load_head(FM_AQ + 64 * h, FM_AK + 64 * h, TM_AV + 64 * h, h)
                moba_select()
                for qb in range(8):
                    o_ = attend(slice(0, 64), qb, 0.125, masked=True)
                    b_ = ob[ocnt[0] % 2]
                    ocnt[0] += 1
                    P.op("act", [o_], [b_], lambda e: e.copy(out=b_[:, :], in_=o_[:, :]))
                    P.dma("pool", kb.yT.t[h * 64:(h + 1) * 64, qb * 512:(qb + 1) * 512], b_[:, :], [b_], [])
        if "b" in mixers:
            sc = 32 ** -0.5
            for h in range(4):
                load_head(FM_BQ + 64 * h, FM_BK + 64 * h, TM_BV + 64 * h, 4 + h)
                for qb in range(8):
                    o1 = attend(slice(0, 32), qb, sc)
                    o2 = attend(slice(32, 64), qb, sc)
                    P.op("dve", [o1, o2, sm], [osb], lambda e: e.scalar_tensor_tensor(out=osb[:, :], in0=o2[:, :], scalar=sm[:, 2:3], in1=o1[:, :], op0=ALU.mult, op1=ALU.add))
                    sq = stmp[0]
                    P.op("act", [osb], [sq], lambda e: e.activation(out=sq[0:64, :], in_=osb[:, :], func=AF.Square))
                    ps = next_ps()
                    P.op("pe", [sq, kb.cst], [ps], lambda e: e.matmul(ps[0:64, :], lhsT=kb.ones[0:64, 0:64], rhs=sq[0:64, :], start=True, stop=True))
                    r_ = stmp[1]
                    P.op("dve", [ps], [r_], lambda e: e.tensor_scalar(out=r_[0:64, :], in0=ps[0:64, :], scalar1=1.0 / 64, scalar2=EPS, op0=ALU.mult, op1=ALU.add))
                    P.op("act", [r_], [r_], lambda e: e.activation(out=r_[0:64, :], in_=r_[0:64, :], func=AF.Sqrt))
                    P.op("dve", [r_], [r_], lambda e: e.reciprocal(out=r_[0:64, :], in_=r_[0:64, :]))
                    b_ = ob[ocnt[0] % 2]
                    ocnt[0] += 1
                    P.op("dve", [osb, r_, sm], [b_], lambda e: e.scalar_tensor_tensor(out=b_[:, :], in0=osb[:, :], scalar=sm[:, 5:6], in1=r_[0:64, :], op0=ALU.mult, op1=ALU.mult))
                    P.dma("pool", kb.yT.t[256 + h * 64:256 + (h + 1) * 64, qb * 512:(qb + 1) * 512], b_[:, :], [b_], [])


def norm_gate_out(kb, P, next_ps, o_ps, gsil, osb, sq, st4, y, ybf, row0, c):
    P.op("act", [o_ps], [osb], lambda e: e.copy(out=osb[:, :], in_=o_ps[:, 0:256]))
    P.op("act", [osb], [sq], lambda e: e.activation(out=sq[:, :], in_=osb[:, :], func=AF.Square))
    P.op("dve", [sq], [st4], lambda e: e.tensor_reduce(out=st4[:, 0:4], in_=sq[:, :].rearrange("p (h d) -> p h d", d=64), axis=AX.X, op=ALU.add))
    P.op("dve", [st4], [st4], lambda e: e.tensor_scalar(out=st4[:, 0:4], in0=st4[:, 0:4], scalar1=1.0 / 64, scalar2=EPS, op0=ALU.mult, op1=ALU.add))
    P.op("act", [st4], [st4], lambda e: e.activation(out=st4[:, 0:4], in_=st4[:, 0:4], func=AF.Sqrt))
    P.op("dve", [st4], [st4], lambda e: e.reciprocal(out=st4[:, 0:4], in_=st4[:, 0:4]))
    for h in range(4):
        hs = slice(h * 64, (h + 1) * 64)
        P.op("dve", [osb, st4, gsil], [y], lambda e: e.scalar_tensor_tensor(out=y[:, hs], in0=osb[:, hs], scalar=st4[:, h:h + 1], in1=gsil[:, hs], op0=ALU.mult, op1=ALU.mult))
    pt = next_ps()
    for j in range(2):
        P.op("pe", [y, kb.cst], [pt], lambda e: e.transpose(pt[:, j * 128:(j + 1) * 128], y[:, j * 128:(j + 1) * 128], kb.ident))
    P.op("act", [pt], [ybf], lambda e: e.copy(out=ybf[:, :], in_=pt[:, 0:256]))
    for j in range(2):
        P.dma("pool", kb.yT.t[row0 + j * 128:row0 + (j + 1) * 128, c * 128:(c + 1) * 128], ybf[:, j * 128:(j + 1) * 128], [ybf], [])


def gla_phase(kb, l):
    P, nc, next_ps = kb.P, kb.nc, kb.next_ps
    U = kb.cst.t[:, 2, :]
    with contextlib.ExitStack() as st:
        qT = P.sbuf(st, [128, S], F32, "gqT")
        kT = P.sbuf(st, [128, S], F32, "gkT")
        lrT = P.sbuf(st, [17, S], F32, "glr")
        waug = P.sbuf(st, [17, 128], F32, "waug")
        Uneg = P.sbuf(st, [128, 128], F32, "Uneg")
        Aneg = P.sbuf(st, [128, 128], F32, "Aneg")
        gainb = P.sbuf(st, [128, 256], F32, "gainb")
        hm = P.sbuf(st, [128, 8], F32, "hm")
        Sf = P.sbuf(st, [128, 256], F32, "Sf")
        Sb = P.sbuf(st, [128, 256], BF16, "Sb")
        nb = 2
        kk = [P.sbuf(st, [128, 128], F32, "kk") for _ in range(nb)]
        vv = [P.sbuf(st, [128, 256], F32, "vv") for _ in range(nb)]
        gg = [P.sbuf(st, [128, 256], F32, "gg") for _ in range(nb)]
        vb = [P.sbuf(st, [128, 256], BF16, "vb") for _ in range(nb)]
        lnr = [P.sbuf(st, [128, 128], F32, "lnr") for _ in range(nb)]
        eb = [P.sbuf(st, [128, 128], F32, "eb") for _ in range(nb)]
        enb = [P.sbuf(st, [128, 128], F32, "enb") for _ in range(nb)]
        bsb = [P.sbuf(st, [128, 128], F32, "bsb") for _ in range(nb)]
        qsm = [P.sbuf(st, [128, 4, 128], BF16, "qsm") for _ in range(nb)]
        ksT = [P.sbuf(st, [128, 128], BF16, "ksT") for _ in range(nb)]
        kdec = [P.sbuf(st, [128, 128], BF16, "kdec") for _ in range(nb)]
        Am = [P.sbuf(st, [128, 4, 128], BF16, "Am") for _ in range(nb)]
        gl = [P.sbuf(st, [128, 1], F32, "gl") for _ in range(nb)]
        gsil = [P.sbuf(st, [128, 256], F32, "gsil") for _ in range(nb)]
        osb = [P.sbuf(st, [128, 256], F32, "osb") for _ in range(nb)]
        sq = [P.sbuf(st, [128, 256], F32, "sq") for _ in range(nb)]
        st4 = [P.sbuf(st, [128, 4], F32, "st4") for _ in range(nb)]
        y = [P.sbuf(st, [128, 256], F32, "y") for _ in range(nb)]
        ybf = [P.sbuf(st, [128, 256], BF16, "ybf") for _ in range(nb)]

        P.dma("sp", qT[:, :], kb.FMS.t[FM_DQ:FM_DQ + 128, :], [], [qT])
        P.dma("pool", kT[:, :], kb.FMS.t[FM_DK:FM_DK + 128, :], [], [kT])
        P.op("dve", [], [lrT], lambda e: e.memset(lrT[:, :], 1.0))
        P.dma("sp", lrT[0:16, :], kb.FMS.t[FM_DLR:FM_DLR + 16, :], [], [lrT])
        P.dma("sp", waug[0:16, :], kb.walpha.t[l, :, :], [], [waug])
        P.dma("sp", waug[16:17, :], kb.balpha.t[l:l + 1, :], [], [waug])
        for h in range(4):
            P.dma("sp", gainb[:, h * 64:(h + 1) * 64], kb.glanorm.t[l:l + 1, :].broadcast_to([128, 64]), [], [gainb])
        P.dma("sp", hm[:, :], kb.hmask.t[:, :], [], [hm])
        P.op("dve", [kb.cst], [Uneg], lambda e: e.tensor_scalar(out=Uneg[:, :], in0=U, scalar1=-1.0 / 16, scalar2=None, op0=ALU.mult))
        P.op("dve", [kb.cst], [Aneg], lambda e: e.tensor_scalar(out=Aneg[:, :], in0=kb.ones, scalar1=-1.0 / 16, scalar2=None, op0=ALU.mult))
        P.op("dve", [], [Sf], lambda e: e.memset(Sf[:, :], 0.0))
        P.op("pool", [], [Sb], lambda e: e.memset(Sb[:, :], 0.0))

        for c in range(S // 128):
            i = c % nb
            sl = slice(c * 128, (c + 1) * 128)
            kk_, vv_, gg_, vb_, lnr_, eb_, enb_, bsb_ = kk[i], vv[i], gg[i], vb[i], lnr[i], eb[i], enb[i], bsb[i]
            qsm_, ksT_, kdec_, Am_, gl_, gsil_ = qsm[i], ksT[i], kdec[i], Am[i], gl[i], gsil[i]
            P.dma("sp", kk_[:, :], kb.TMS.t[sl, TM_DK:TM_DK + 128], [], [kk_])
            P.dma("sp", vv_[:, :], kb.TMS.t[sl, TM_DV:TM_DV + 256], [], [vv_])
            P.dma("sp", gg_[:, :], kb.TMS.t[sl, TM_DG:TM_DG + 256], [], [gg_])
            P.op("pool", [vv_], [vb_], lambda e: e.tensor_copy(out=vb_[:, :], in_=vv_[:, :]))
            P.op("act", [gg_], [gsil_], lambda e: e.activation(out=gsil_[:, :], in_=gg_[:, :], func=AF.Silu))
            P.op("pool", [gsil_, gainb], [gsil_], lambda e: e.tensor_tensor(out=gsil_[:, :], in0=gsil_[:, :], in1=gainb[:, :], op=ALU.mult))
            pre = next_ps()
            P.op("pe", [lrT, waug], [pre], lambda e: e.matmul(pre[:, 0:128], lhsT=lrT[0:17, sl], rhs=waug[0:17, :], start=True, stop=True))
            P.op("act", [pre], [lnr_], lambda e: e.activation(out=lnr_[:, :], in_=pre[:, 0:128], func=AF.Exp, scale=-1.0))
            P.op("act", [lnr_], [lnr_], lambda e: e.activation(out=lnr_[:, :], in_=lnr_[:, :], func=AF.Ln, bias=1.0))
            pb_ = next_ps()
            P.op("pe", [Uneg, lnr_], [pb_], lambda e: e.matmul(pb_[:, 0:128], lhsT=Uneg[:, :], rhs=lnr_[:, :], start=True, stop=True))
            P.op("pe", [Aneg, lnr_], [pb_], lambda e: e.matmul(pb_[:, 128:256], lhsT=Aneg[:, :], rhs=lnr_[:, :], start=True, stop=True))
            P.op("pe", [Uneg, lnr_], [pb_], lambda e: e.matmul(pb_[:, 256:384], lhsT=lnr_[:, :], rhs=Uneg[:, :], start=True, stop=True))
            P.op("pe", [Aneg, lnr_], [pb_], lambda e: e.matmul(pb_[:, 384:385], lhsT=lnr_[:, :], rhs=Aneg[:, 0:1], start=True, stop=True))
            P.op("act", [pb_], [eb_], lambda e: e.activation(out=eb_[:, :], in_=pb_[:, 256:384], func=AF.Exp))
            P.op("act", [pb_], [enb_], lambda e: e.activation(out=enb_[:, :], in_=pb_[:, 256:384], func=AF.Exp, scale=-1.0))
            P.op("act", [pb_], [gl_], lambda e: e.activation(out=gl_[:, :], in_=pb_[:, 384:385], func=AF.Exp))
            P.op("act", [pb_], [bsb_], lambda e: e.copy(out=bsb_[:, :], in_=pb_[:, 0:128]))
            P.op("dve", [pb_, bsb_], [bsb_], lambda e: e.tensor_tensor(out=bsb_[:, :], in0=pb_[:, 128:256], in1=bsb_[:, :], op=ALU.subtract))
            P.op("act", [bsb_], [bsb_], lambda e: e.activation(out=bsb_[:, :], in_=bsb_[:, :], func=AF.Exp))
            P.op("dve", [kk_, bsb_], [kdec_], lambda e: e.tensor_tensor(out=kdec_[:, :], in0=kk_[:, :], in1=bsb_[:, :], op=ALU.mult))
            for h in range(4):
                P.op("dve", [qT, hm, eb_], [qsm_], lambda e: e.scalar_tensor_tensor(out=qsm_[:, h, :], in0=qT[:, sl], scalar=hm[:, h:h + 1], in1=eb_[:, :], op0=ALU.mult, op1=ALU.mult))
            P.op("dve", [kT, enb_], [ksT_], lambda e: e.tensor_tensor(out=ksT_[:, :], in0=kT[:, sl], in1=enb_[:, :], op=ALU.mult))
            pa = next_ps()
            for h in range(4):
                P.op("pe", [ksT_, qsm_], [pa], lambda e: e.matmul(pa[:, h * 128:(h + 1) * 128], lhsT=ksT_[:, :], rhs=qsm_[:, h, :], start=True, stop=True))
            for h in range(4):
                eng = "dve" if h % 2 == 0 else "pool"
                if eng == "pool":
                    eng = "dve"
                P.op(eng, [pa, kb.cst], [Am_], lambda e: e.tensor_tensor(out=Am_[:, h, :], in0=pa[:, h * 128:(h + 1) * 128], in1=U, op=ALU.mult))
            po = next_ps()
            for h in range(4):
                hs = slice(h * 64, (h + 1) * 64)
                P.op("pe", [qsm_, Sb], [po], lambda e: e.matmul(po[:, hs], lhsT=qsm_[:, h, :], rhs=Sb[:, hs], start=True, stop=False))
                P.op("pe", [Am_, vb_], [po], lambda e: e.matmul(po[:, hs], lhsT=Am_[:, h, :], rhs=vb_[:, hs], start=False, stop=True))
            pd = next_ps()
            P.op("pe", [kdec_, vb_], [pd], lambda e: e.matmul(pd[:, 0:256], lhsT=kdec_[:, :], rhs=vb_[:, :], start=True, stop=True))
            P.op("dve", [Sf, gl_, pd], [Sf], lambda e: e.scalar_tensor_tensor(out=Sf[:, :], in0=Sf[:, :], scalar=gl_[:, 0:1], in1=pd[:, 0:256], op0=ALU.mult, op1=ALU.add))
            P.op("act", [Sf], [Sb], lambda e: e.copy(out=Sb[:, :], in_=Sf[:, :]))
            norm_gate_out(kb, P, next_ps, po, gsil_, osb[i], sq[i], st4[i], y[i], ybf[i], 768, c)


def gdn_phase(kb, l):
    P, nc, next_ps = kb.P, kb.nc, kb.next_ps
    ident, ones = kb.ident, kb.ones
    U = kb.cst.t[:, 2, :]
    SU = kb.cst.t[:, 3, :]
    SL = kb.cst.t[:, 4, :]
    BD = kb.cst.t[:, 5, :]
    import os as _os
    NCH = int(_os.environ.get("GDN_NCH", S // 128))
    GST = int(_os.environ.get("GDN_STAGE", 9))
    with contextlib.ExitStack() as st:
        qTn = P.sbuf(st, [128, 2, S], F32, "cqT")
        kTn = P.sbuf(st, [128, 2, S], F32, "ckT")
        vTc = P.sbuf(st, [128, 2, S], F32, "cvT")
        cw = P.sbuf(st, [128, 6, 4], F32, "cw")
        P.dma("sp", cw[:, :, :], kb.gconv.t[l, :, :, :], [], [cw])
        with contextlib.ExitStack() as st0:
            xp = P.sbuf(st0, [128, S + 3], F32, "xp")
            accs = [P.sbuf(st0, [128, S], F32, "cacc") for _ in range(2)]
            sqt = [P.sbuf(st0, [128, 512], F32, "csq") for _ in range(2)]
            rst = [P.sbuf(st0, [128, 512], F32, "crs") for _ in range(2)]
            P.op("pool", [], [xp], lambda e: e.memset(xp[:, 0:3], 0.0))
            for j in range(6):
                ac = accs[j % 2]
                P.dma("sp", xp[:, 3:S + 3], kb.FMS.t[FM_C + j * 128:FM_C + (j + 1) * 128, :], [], [xp])
                P.op("dve", [xp, cw], [ac], lambda e: e.tensor_scalar(out=ac[:, :], in0=xp[:, 0:S], scalar1=cw[:, j, 0:1], scalar2=None, op0=ALU.mult))
                for i in range(1, 4):
                    P.op("dve", [xp, cw, ac], [ac], lambda e: e.scalar_tensor_tensor(out=ac[:, :], in0=xp[:, i:S + i], scalar=cw[:, j, i:i + 1], in1=ac[:, :], op0=ALU.mult, op1=ALU.add))
                dst = (qTn, kTn, vTc)[j // 2]
                jj = j % 2
                if j >= 4:
                    P.op("act", [ac], [dst], lambda e: e.activation(out=dst[:, jj, :], in_=ac[:, :], func=AF.Silu))
                    continue
                P.op("act", [ac], [ac], lambda e: e.activation(out=ac[:, :], in_=ac[:, :], func=AF.Silu))
                qscale = 0.125 if j < 2 else 1.0
                for sb in range(S // 512):
                    sl = slice(sb * 512, (sb + 1) * 512)
                    sq_, rs_ = sqt[sb % 2], rst[sb % 2]
                    P.op("act", [ac], [sq_], lambda e: e.activation(out=sq_[:, :], in_=ac[:, sl], func=AF.Square))
                    ps = next_ps()
                    P.op("pe", [sq_, kb.cst], [ps], lambda e: e.matmul(ps[:, :], lhsT=BD, rhs=sq_[:, :], start=True, stop=True))
                    P.op("dve", [ps], [rs_], lambda e: e.tensor_scalar(out=rs_[:, :], in0=ps[:, :], scalar1=EPS, scalar2=None, op0=ALU.add))
                    P.op("act", [rs_], [rs_], lambda e: e.activation(out=rs_[:, :], in_=rs_[:, :], func=AF.Sqrt))
                    P.op("dve", [rs_], [rs_], lambda e: e.reciprocal(out=rs_[:, :], in_=rs_[:, :]))
                    P.op("dve", [ac, rs_], [dst], lambda e: e.scalar_tensor_tensor(out=dst[:, jj, sl], in0=ac[:, sl], scalar=qscale, in1=rs_[:, :], op0=ALU.mult, op1=ALU.mult))
            P.barrier()
        if GST == 0:
            return
        c128 = lambda nm, dt=F32: P.sbuf(st, [128, 4, 128], dt, nm)
        I4, SL4 = c128("I4"), c128("SL4")
        NEGU = P.sbuf(st, [128, 128], F32, "NEGU")
        NEGL = P.sbuf(st, [128, 128], F32, "NEGL")
        for h in range(4):
            P.op("pool", [kb.cst], [I4], lambda e: e.tensor_copy(out=I4[:, h, :], in_=ident))
            P.op("pool", [kb.cst], [SL4], lambda e: e.tensor_copy(out=SL4[:, h, :], in_=SL))
        P.op("dve", [kb.cst], [NEGU], lambda e: e.tensor_scalar(out=NEGU[:, :], in0=SU, scalar1=-BIG, scalar2=None, op0=ALU.mult))
        P.op("dve", [kb.cst], [NEGL], lambda e: e.tensor_scalar(out=NEGL[:, :], in0=SL, scalar1=-BIG, scalar2=None, op0=ALU.mult))
        gainb = P.sbuf(st, [128, 256], F32, "cgainb")
        for h in range(4):
            P.dma("sp", gainb[:, h * 64:(h + 1) * 64], kb.gdnnorm.t[l:l + 1, :].broadcast_to([128, 64]), [], [gainb])
        ab = P.sbuf(st, [128, 8], F32, "cab")
        P.dma("sp", ab[:, 0:4], kb.gdtb.t[l:l + 1, :].broadcast_to([128, 4]), [], [ab])
        P.dma("sp", ab[:, 4:8], kb.galog.t[l:l + 1, :].broadcast_to([128, 4]), [], [ab])
        P.op("act", [ab], [ab], lambda e: e.activation(out=ab[:, 4:8], in_=ab[:, 4:8], func=AF.Exp))
        P.op("dve", [ab], [ab], lambda e: e.tensor_scalar(out=ab[:, 4:8], in0=ab[:, 4:8], scalar1=-1.0, scalar2=None, op0=ALU.mult))
        S4 = P.sbuf(st, [64, 4, 64], F32, "S4")
        P.op("dve", [], [S4], lambda e: e.memset(S4[:, :, :], 0.0))
        nb = 2
        mk = lambda shape, nm, dt=F32: [P.sbuf(st, shape, dt, nm) for _ in range(nb)]
        cz, cabt, sc8 = mk([128, 256], "cz"), mk([128, 8], "cabt"), mk([128, 24], "sc8")
        kt, vt, kbt, vbt, kbg, kdec = (mk([128, 256], n) for n in ("kt", "vt", "kbt", "vbt", "kbg", "kdec"))
        q4, k4 = mk([64, 4, 128], "q4"), mk([64, 4, 128], "k4")
        Ug4, nUg4, D4, DT4, N4, M4, P4, qkT4 = (mk([128, 4, 128], n) for n in ("Ug4", "nUg4", "D4", "DT4", "N4", "M4", "P4", "qkT4"))
        usb, vnew, o2sb, osb = (mk([128, 256], n) for n in ("usb", "vnew", "o2sb", "cosb"))
        wT4 = mk([64, 4, 128], "wT4")
        gsil, osb2, sq, y = (mk([128, 256], n) for n in ("cgsil", "cosb2", "csq2", "cy"))
        st4 = mk([128, 4], "cst4")
        ybf = mk([128, 256], "cybf", BF16)

        try:
            print("GDN sbuf remaining", nc.sbuf_bytes_remaining, flush=True)
        except Exception as ex:
            print("sbuf_bytes_remaining failed", ex)

        def bc(ap4):
            return ap4.unsqueeze(2).broadcast_to([128, 4, 64])

        v3 = lambda t: t[:, :].rearrange("p (h d) -> p h d", d=64)

        for c in range(NCH):
            i = c % nb
            sl = slice(c * 128, (c + 1) * 128)
            cz_, cab_, sc_ = cz[i], cabt[i], sc8[i]
            kt_, vt_, kbt_, vbt_, kbg_, kdec_ = kt[i], vt[i], kbt[i], vbt[i], kbg[i], kdec[i]
            q4_, k4_, Ug_, nUg_, D_, DT_, N_, M_, P_, qk_ = q4[i], k4[i], Ug4[i], nUg4[i], D4[i], DT4[i], N4[i], M4[i], P4[i], qkT4[i]
            usb_, vnew_, o2sb_, osb_, wT_, gsil_ = usb[i], vnew[i], o2sb[i], osb[i], wT4[i], gsil[i]
            P.dma("sp", cz_[:, :], kb.TMS.t[sl, TM_CZ:TM_CZ + 256], [], [cz_])
            P.dma("sp", cab_[:, :], kb.TMS.t[sl, TM_CA:TM_CA + 8], [], [cab_])
            P.op("act", [cz_], [gsil_], lambda e: e.activation(out=gsil_[:, :], in_=cz_[:, :], func=AF.Silu))
            P.op("pool", [gsil_, gainb], [gsil_], lambda e: e.tensor_tensor(out=gsil_[:, :], in0=gsil_[:, :], in1=gainb[:, :], op=ALU.mult))
            P.op("dve", [cab_, ab], [sc_], lambda e: e.tensor_tensor(out=sc_[:, 0:4], in0=cab_[:, 0:4], in1=ab[:, 0:4], op=ALU.add))
            P.op("act", [sc_], [sc_], lambda e: e.activation(out=sc_[:, 0:4], in_=sc_[:, 0:4], func=AF.Exp))
            P.op("act", [sc_], [sc_], lambda e: e.activation(out=sc_[:, 0:4], in_=sc_[:, 0:4], func=AF.Ln, bias=1.0))
            P.op("dve", [sc_, ab], [sc_], lambda e: e.tensor_tensor(out=sc_[:, 0:4], in0=sc_[:, 0:4], in1=ab[:, 4:8], op=ALU.mult))
            P.op("act", [cab_], [sc_], lambda e: e.activation(out=sc_[:, 4:8], in_=cab_[:, 4:8], func=AF.Sigmoid))
            pg = next_ps()
            P.op("pe", [sc_, kb.cst], [pg], lambda e: e.matmul(pg[:, 0:4], lhsT=U, rhs=sc_[:, 0:4], start=True, stop=True))
            P.op("pe", [sc_, kb.cst], [pg], lambda e: e.matmul(pg[:, 4:8], lhsT=SL, rhs=sc_[:, 0:4], start=True, stop=True))
            P.op("pe", [sc_, kb.cst], [pg], lambda e: e.matmul(pg[:, 8:12], lhsT=ones, rhs=sc_[:, 0:4], start=True, stop=True))
            P.op("act", [pg], [sc_], lambda e: e.activation(out=sc_[:, 8:20], in_=pg[:, 0:12], func=AF.Exp))
            if GST == 1:
                continue
            ptk = next_ps()
            for j in range(2):
                P.op("pe", [kTn, kb.cst], [ptk], lambda e: e.transpose(ptk[:, j * 128:(j + 1) * 128], kTn[:, j, sl], ident))
                P.op("pe", [vTc, kb.cst], [ptk], lambda e: e.transpose(ptk[:, 256 + j * 128:256 + (j + 1) * 128], vTc[:, j, sl], ident))
            P.op("act", [ptk], [kt_], lambda e: e.copy(out=kt_[:, :], in_=ptk[:, 0:256]))
            P.op("dve", [ptk], [vt_], lambda e: e.tensor_copy(out=vt_[:, :], in_=ptk[:, 256:512]))
            for h in range(4):
                pb, j = 64 * (h % 2), h // 2
                P.op("dve", [qTn], [q4_], lambda e: e.tensor_copy(out=q4_[:, h, :], in_=qTn[pb:pb + 64, j, sl]))
                P.op("dve", [kTn], [k4_], lambda e: e.tensor_copy(out=k4_[:, h, :], in_=kTn[pb:pb + 64, j, sl]))
            P.op("dve", [kt_, sc_], [kbt_], lambda e: e.tensor_tensor(out=v3(kbt_), in0=v3(kt_), in1=bc(sc_[:, 4:8]), op=ALU.mult))
            P.op("dve", [vt_, sc_], [vbt_], lambda e: e.tensor_tensor(out=v3(vbt_), in0=v3(vt_), in1=bc(sc_[:, 4:8]), op=ALU.mult))
            P.op("dve", [kbt_, sc_], [kbg_], lambda e: e.tensor_tensor(out=v3(kbg_), in0=v3(kbt_), in1=bc(sc_[:, 8:12]), op=ALU.mult))
            P.op("dve", [kt_, sc_], [kdec_], lambda e: e.tensor_tensor(out=v3(kdec_), in0=v3(kt_), in1=bc(sc_[:, 12:16]), op=ALU.mult))
            for h in range(4):
                P.op("dve", [sc_, kb.cst], [Ug_], lambda e: e.tensor_scalar(out=Ug_[:, h, :], in0=U, scalar1=sc_[:, h:h + 1], scalar2=None, op0=ALU.mult))
            P.op("dve", [Ug_], [nUg_], lambda e: e.tensor_scalar(out=nUg_[:, :, :], in0=Ug_[:, :, :], scalar1=-1.0, scalar2=None, op0=ALU.mult))
            if GST == 2:
                continue
            pa, pat = next_ps(), next_ps()
            for h in range(4):
                hs = slice(h * 128, (h + 1) * 128)
                P.op("pe", [Ug_, kb.cst], [pa], lambda e: e.matmul(pa[:, hs], lhsT=Ug_[:, h, :], rhs=ones, start=True, stop=False))
                P.op("pe", [nUg_, kb.cst], [pa], lambda e: e.matmul(pa[:, hs], lhsT=ones, rhs=nUg_[:, h, :], start=False, stop=False))
                P.op("pe", [NEGU, kb.cst], [pa], lambda e: e.matmul(pa[:, hs], lhsT=ident, rhs=NEGU[:, :], start=False, stop=True))
                P.op("pe", [Ug_, kb.cst], [pat], lambda e: e.matmul(pat[:, hs], lhsT=ones, rhs=Ug_[:, h, :], start=True, stop=False))
                P.op("pe", [nUg_, kb.cst], [pat], lambda e: e.matmul(pat[:, hs], lhsT=nUg_[:, h, :], rhs=ones, start=False, stop=False))
                P.op("pe", [NEGL, kb.cst], [pat], lambda e: e.matmul(pat[:, hs], lhsT=ident, rhs=NEGL[:, :], start=False, stop=True))
            f4 = lambda t: t[:, :, :].rearrange("p h n -> p (h n)")
            P.op("act", [pa], [D_], lambda e: e.activation(out=f4(D_), in_=pa[:, :], func=AF.Exp))
            P.op("act", [pat], [DT_], lambda e: e.activation(out=f4(DT_), in_=pat[:, :], func=AF.Exp))
            if GST == 3:
                continue
            pgm, pqk = next_ps(), next_ps()
            for h in range(4):
                hs = slice(h * 128, (h + 1) * 128)
                P.op("pe", [k4_], [pgm], lambda e: e.matmul(pgm[:, hs], lhsT=k4_[:, h, :], rhs=k4_[:, h, :], start=True, stop=True))
                P.op("pe", [k4_, q4_], [pqk], lambda e: e.matmul(pqk[:, hs], lhsT=k4_[:, h, :], rhs=q4_[:, h, :], start=True, stop=True))
            SUB = int(_os.environ.get("GDN_SUB", 9))
            if SUB == 0:
                continue
            for h in range(4):
                hs = slice(h * 128, (h + 1) * 128)
                P.op("dve", [pgm, sc_, D_], [N_], lambda e: e.scalar_tensor_tensor(out=N_[:, h, :], in0=pgm[:, hs], scalar=sc_[:, 4 + h:5 + h], in1=D_[:, h, :], op0=ALU.mult, op1=ALU.mult))
            if SUB == 1:
                continue
            P.op("dve", [N_, SL4], [N_], lambda e: e.tensor_tensor(out=f4(N_), in0=f4(N_), in1=f4(SL4), op=ALU.mult))
            P.op("dve", [pqk, DT_], [qk_], lambda e: e.tensor_tensor(out=f4(qk_), in0=pqk[:, :], in1=f4(DT_), op=ALU.mult))
            if SUB == 2:
                continue
            pm = next_ps()
            for h in range(4):
                P.op("pe", [N_, kb.cst], [pm], lambda e: e.transpose(pm[:, h * 128:(h + 1) * 128], N_[:, h, :], ident))
            if SUB == 3:
                continue
            P.op("act", [pm], [M_], lambda e: e.copy(out=f4(M_), in_=pm[:, :]))
            if SUB == 4:
                continue
            PV = int(_os.environ.get("GDN_PV", 3))
            if PV == 1:
                P.op("dve", [pm, I4], [P_], lambda e: e.scalar_tensor_tensor(out=f4(P_), in0=pm[:, :], scalar=-1.0, in1=f4(I4), op0=ALU.mult, op1=ALU.add))
            elif PV == 2:
                P.op("act", [pm], [P_], lambda e: e.mul(out=f4(P_), in_=pm[:, :], mul=-1.0))
                P.op("pool", [P_, I4], [P_], lambda e: e.tensor_tensor(out=f4(P_), in0=f4(P_), in1=f4(I4), op=ALU.add))
            elif PV == 3:
                P.op("dve", [M_, I4], [P_], lambda e: e.tensor_tensor(out=f4(P_), in0=f4(I4), in1=f4(M_), op=ALU.subtract))
            if GST == 4:
                continue
            for r in range(6):
                last = (r == 5)
                pn = next_ps()
                for h in range(4):
                    P.op("pe", [M_, N_], [pn], lambda e: e.matmul(pn[:, h * 128:(h + 1) * 128], lhsT=M_[:, h, :], rhs=N_[:, h, :], start=True, stop=True))
                if not last:
                    pm2 = next_ps()
                    for h in range(4):
                        P.op("pe", [M_, N_], [pm2], lambda e: e.matmul(pm2[:, h * 128:(h + 1) * 128], lhsT=N_[:, h, :], rhs=M_[:, h, :], start=True, stop=True))
                P.op("dve", [pn], [N_], lambda e: e.tensor_copy(out=f4(N_), in_=pn[:, :]))
                if not last:
                    P.op("act", [pm2], [M_], lambda e: e.copy(out=f4(M_), in_=pm2[:, :]))
                pp = next_ps()
                for h in range(4):
                    P.op("pe", [N_, P_], [pp], lambda e: e.matmul(pp[:, h * 128:(h + 1) * 128], lhsT=N_[:, h, :], rhs=P_[:, h, :], start=True, stop=True))
                P.op("dve", [pp, P_], [P_], lambda e: e.tensor_tensor(out=f4(P_), in0=pp[:, :], in1=f4(P_), op=ALU.add))
            if GST == 5:
                continue
            pu, pw = next_ps(), next_ps()
            for h in range(4):
                P.op("pe", [P_, vbt_], [pu], lambda e: e.matmul(pu[:, h * 64:(h + 1) * 64], lhsT=P_[:, h, :], rhs=vbt_[:, h * 64:(h + 1) * 64], start=True, stop=True))
                P.op("pe", [P_, kbg_], [pw], lambda e: e.matmul(pw[0:64, h * 128:(h + 1) * 128], lhsT=kbg_[:, h * 64:(h + 1) * 64], rhs=P_[:, h, :], start=True, stop=True))
            P.op("act", [pu], [usb_], lambda e: e.copy(out=usb_[:, :], in_=pu[:, 0:256]))
            P.op("dve", [pw], [wT_], lambda e: e.tensor_copy(out=wT_[:, :, :].rearrange("p h n -> p (h n)"), in_=pw[0:64, :]))
            if GST == 6:
                continue
            pws, po1 = next_ps(), next_ps()
            for h in range(4):
                hs = slice(h * 64, (h + 1) * 64)
                P.op("pe", [wT_, S4], [pws], lambda e: e.matmul(pws[:, hs], lhsT=wT_[:, h, :], rhs=S4[:, h, :], start=True, stop=True))
                P.op("pe", [q4_, S4], [po1], lambda e: e.matmul(po1[:, hs], lhsT=q4_[:, h, :], rhs=S4[:, h, :], start=True, stop=True))
            P.op("dve", [usb_, pws], [vnew_], lambda e: e.scalar_tensor_tensor(out=vnew_[:, :], in0=pws[:, 0:256], scalar=-1.0, in1=usb_[:, :], op0=ALU.mult, op1=ALU.add))
            po2, pd = next_ps(), next_ps()
            for h in range(4):
                hs = slice(h * 64, (h + 1) * 64)
                P.op("pe", [qk_, vnew_], [po2], lambda e: e.matmul(po2[:, hs], lhsT=qk_[:, h, :], rhs=vnew_[:, hs], start=True, stop=True))
                P.op("pe", [kdec_, vnew_], [pd], lambda e: e.matmul(pd[0:64, hs], lhsT=kdec_[:, hs], rhs=vnew_[:, hs], start=True, stop=True))
            P.op("act", [po2], [o2sb_], lambda e: e.copy(out=o2sb_[:, :], in_=po2[:, 0:256]))
            for h in range(4):
                hs = slice(h * 64, (h + 1) * 64)
                P.op("dve", [po1, sc_, o2sb_], [osb_], lambda e: e.scalar_tensor_tensor(out=osb_[:, hs], in0=po1[:, hs], scalar=sc_[:, 8 + h:9 + h], in1=o2sb_[:, hs], op0=ALU.mult, op1=ALU.add))
                P.op("dve", [S4, sc_, pd], [S4], lambda e: e.scalar_tensor_tensor(out=S4[:, h, :], in0=S4[:, h, :], scalar=sc_[0:64, 16 + h:17 + h], in1=pd[0:64, hs], op0=ALU.mult, op1=ALU.add))
            norm_gate_out(kb, P, next_ps, osb_, gsil_, osb2[i], sq[i], st4[i], y[i], ybf[i], 512, c)


def _rel_bucket_np(dist):
    n = np.maximum(dist, 0)
    nf = np.maximum(n, 1).astype(np.float32)
    large = 16 + (np.log(nf / 16) / math.log(128 / 16) * 16).astype(np.int32)
    large = np.minimum(large, 31)
    return np.where(n < 16, n, large)


def _host_consts():
    c = np.zeros((128, 6, 128), np.float32)
    c[:, 4, :] = np.tril(np.ones((128, 128), np.float32), -1)
    c[:, 5, :] = np.kron(np.eye(2, dtype=np.float32), np.ones((64, 64), np.float32))
    c[:, 0, :] = np.eye(128, dtype=np.float32)
    c[:, 1, :] = 1.0
    c[:, 2, :] = np.triu(np.ones((128, 128), np.float32))
    c[:, 3, :] = np.triu(np.ones((128, 128), np.float32), 1)
    return c


def prep_inputs(inputs, n_layers=DEPTH):
    f = lambda k: np.asarray(inputs[k], dtype=np.float32)
    perm = _col_perm()
    shared = {
        "w_in": np.ascontiguousarray(f("w_in")[:, :, perm]),
        "w_out": f("w_out"), "w_gate": f("w_gate"), "w_up": f("w_up"), "w_down": f("w_down"),
        "w_pg": f("w_ple_gate"), "w_pp": f("w_ple_proj"),
        "consts": _host_consts(),
    }
    import ml_dtypes
    shared["ind16"] = (np.arange(S)[None, :] // 256 == np.arange(16)[:, None]).astype(ml_dtypes.bfloat16)
    mcst = np.zeros((128, 3, 32, 16), np.float32)
    tt = np.arange(32)[:, None] // 2
    nn = np.arange(16)[None, :]
    mcst[:, 0] = np.where(nn < tt, 0.0, -1e30)
    mcst[:, 1] = (nn < tt)
    mcst[:, 2] = (nn == tt)
    shared["mconst"] = mcst.reshape(128, 3, 512)
    shared["gconv"] = np.ascontiguousarray(f("gdn_conv").reshape(DEPTH, 4, 6, 128).transpose(0, 3, 2, 1))
    shared["gdtb"] = f("gdn_dt_bias")
    shared["galog"] = f("gdn_a_log")
    shared["walpha"] = f("gla_w_alpha")
    shared["balpha"] = f("gla_b_alpha")
    shared["glanorm"] = f("gla_norm")
    shared["gdnnorm"] = f("gdn_norm")
    hmk = np.zeros((128, 8), np.float32)
    for h in range(4):
        hmk[32 * h:32 * (h + 1), h] = 32 ** -0.5
        hmk[64 * (h % 2):64 * (h % 2 + 1), 4 + h] = 1.0
    shared["hmask"] = hmk
    shared["dlam"] = np.ascontiguousarray(f("diff_lambda").reshape(DEPTH, 128))
    shared["dnorm"] = np.ascontiguousarray(f("diff_norm").reshape(DEPTH, 64, 1))
    g = np.zeros((DEPTH * 3 + 1, D), np.float32)
    for l in range(DEPTH):
        g[l * 3 + 0] = f("norm_mix")[l]
        g[l * 3 + 1] = f("norm_ffn")[l]
        g[l * 3 + 2] = f("norm_ple")[l]
    g[DEPTH * 3] = f("final_norm")
    shared["gains"] = np.ascontiguousarray(g.reshape(DEPTH * 3 + 1, KC, 128).transpose(2, 0, 1))
    kk = np.arange(128)[:, None]
    cc = np.arange(1024)[None, :]
    dist = cc - 384 - kk
    bidx = _rel_bucket_np(dist)
    rb = f("rel_bias")
    gt = np.empty((8, 128, 1024), np.float32)
    for h in range(8):
        gt[h] = np.where(dist < 0, np.float32(-BIG), rb[bidx, h])
    shared["gtab"] = gt
    x = f("x")
    p = f("p")
    maps = []
    for b in range(x.shape[0]):
        m = dict(shared)
        m["xT"] = np.ascontiguousarray(x[b].T)
        m["pT"] = np.ascontiguousarray(p[:, b].transpose(0, 2, 1))
        maps.append(m)
    return maps


_NC_CACHE = {}


def kernel(**inputs):
    maps = prep_inputs(inputs)
    if "nc" not in _NC_CACHE:
        _NC_CACHE["nc"] = build_program()
    nc = _NC_CACHE["nc"]
    res = run_bass_kernel_spmd(nc, maps, core_ids=list(range(8)))
    out = np.stack([np.ascontiguousarray(r["outT"].T) for r in res.results], axis=0)
    return out.astype(np.float32)
```
